# Optimizing a Trainium2 kernel written in Bass

```python
import math
import jax, jax.numpy as jnp
from jax import lax
import numpy as np

D_MODEL = 1024
BATCH = 4
SEQ = 8192
DEPTH = 2

CHUNK = 64
N_MIXERS = 2
MLSTM_HEADS = 4
MLSTM_DK = 128
MLSTM_DV = 256
CONV_W = 4
POOL_WINDOWS = (2, 4, 8, 16)
POOL_GROUPS = 4
POOL_GW = D_MODEL // POOL_GROUPS
MEM_LEN = 256
XATTN_HEADS = 4
XATTN_HD = D_MODEL // XATTN_HEADS
D_FF = 4 * D_MODEL
N_A = (DEPTH + 1) // 2
N_B = DEPTH // 2
ALPHA = (2 * DEPTH) ** 0.25
BETA = (8 * DEPTH) ** -0.25
LN_EPS = 1e-5

kernel_name = "hybrid_mlstm_pool_memxattn_trunk"


def layer_norm(x, g, b):
    xf = x.astype(jnp.float32)
    mu = jnp.mean(xf, axis=-1, keepdims=True)
    var = jnp.mean(jnp.square(xf - mu), axis=-1, keepdims=True)
    y = (xf - mu) * lax.rsqrt(var + LN_EPS) * g.astype(jnp.float32) + b.astype(jnp.float32)
    return y.astype(x.dtype)


def causal_dwconv(u, w):
    S = u.shape[1]
    up = jnp.pad(u, ((0, 0), (CONV_W - 1, 0), (0, 0)))
    acc = up[:, 0:S] * w[0]
    for j in range(1, CONV_W):
        acc = acc + up[:, j:j + S] * w[j]
    return acc


def mlstm_chunk_step(carry, inp):
    C, n, m = carry
    q, k, v, ig, lf = inp
    L = q.shape[2]
    b = jnp.cumsum(lf, axis=-1)
    tril = jnp.tril(jnp.ones((L, L), dtype=bool))
    dmat = jnp.where(tril, b[..., :, None] - b[..., None, :] + ig[..., None, :], -jnp.inf)
    inter = b + m[..., None]
    m_t = jnp.maximum(inter, jnp.max(dmat, axis=-1))
    wts = jnp.exp(dmat - m_t[..., None])
    sc = jnp.einsum('bhtd,bhsd->bhts', q, k) * wts
    a_t = jnp.exp(inter - m_t)
    num = jnp.einsum('bhts,bhsv->bhtv', sc, v) + a_t[..., None] * jnp.einsum('bhtd,bhdv->bhtv', q, C)
    den = jnp.sum(sc, axis=-1) + a_t * jnp.einsum('bhtd,bhd->bht', q, n)
    h = num / jnp.maximum(jnp.abs(den), jnp.exp(-m_t))[..., None]
    b_last = b[..., -1]
    g = b_last[..., None] - b + ig
    m_new = jnp.maximum(b_last + m, jnp.max(g, axis=-1))
    decay = jnp.exp(b_last + m - m_new)
    wk = jnp.exp(g - m_new[..., None])
    C_new = decay[..., None, None] * C + jnp.einsum('bhs,bhsd,bhsv->bhdv', wk, k, v)
    n_new = decay[..., None] * n + jnp.einsum('bhs,bhsd->bhd', wk, k)
    return (C_new, n_new, m_new), h


def mlstm_mixer(x, w_in, gate_b, conv_w, norm_g, w_out):
    B, S, _ = x.shape
    H, DK, DV, L = MLSTM_HEADS, MLSTM_DK, MLSTM_DV, CHUNK
    NC = S // L
    nqk, nv = 2 * H * DK, H * DV
    proj = x @ w_in
    qk = jax.nn.silu(causal_dwconv(proj[..., :nqk], conv_w))
    v = proj[..., nqk:nqk + nv]
    o = proj[..., nqk + nv:nqk + nv + D_MODEL]
    gates = proj[..., nqk + nv + D_MODEL:].astype(jnp.float32) + gate_b.astype(jnp.float32)
    i_pre = gates[..., :H]
    log_f = jax.nn.log_sigmoid(gates[..., H:])

    def to_chunks(t, d):
        return t.astype(jnp.float32).reshape(B, NC, L, H, d).transpose(1, 0, 3, 2, 4)

    q = to_chunks(qk[..., :H * DK], DK) * (DK ** -0.5)
    k = to_chunks(qk[..., H * DK:], DK)
    vc = to_chunks(v, DV)
    ic = i_pre.reshape(B, NC, L, H).transpose(1, 0, 3, 2)
    fc = log_f.reshape(B, NC, L, H).transpose(1, 0, 3, 2)
    init = (jnp.zeros((B, H, DK, DV), jnp.float32),
            jnp.zeros((B, H, DK), jnp.float32),
            jnp.zeros((B, H), jnp.float32))
    _, h = lax.scan(mlstm_chunk_step, init, (q, k, vc, ic, fc))
    h = h.transpose(1, 0, 3, 2, 4).reshape(B, S, H, DV)
    h = h * lax.rsqrt(jnp.mean(jnp.square(h), axis=-1, keepdims=True) + LN_EPS)
    h = h.reshape(B, S, H * DV) * norm_g.astype(jnp.float32)
    y = (jax.nn.sigmoid(o.astype(jnp.float32)) * h).astype(x.dtype)
    return y @ w_out


def pool_mixer(x, w_grp, scale):
    B, S, D = x.shape
    xg = x.astype(jnp.float32).reshape(B, S, POOL_GROUPS, POOL_GW)
    cs = jnp.cumsum(xg, axis=1)
    t1 = (jnp.arange(S) + 1).astype(jnp.float32)
    pooled = []
    for j, w in enumerate(POOL_WINDOWS):
        c = cs[:, :, j]
        lo = jnp.concatenate([jnp.zeros((B, w, POOL_GW), jnp.float32), c[:, :S - w]], axis=1)
        cnt = jnp.minimum(t1, float(w))[None, :, None]
        pooled.append((c - lo) / cnt)
    pooled = jnp.stack(pooled, axis=2)
    diff = pooled - xg
    y = jnp.einsum('bsgc,gcd->bsgd', diff, w_grp.astype(jnp.float32)).reshape(B, S, D)
    return (y * scale.astype(jnp.float32)).astype(x.dtype)


def memory_cross_attn(x, mem_k, mem_v, wq, wo):
    B, S, _ = x.shape
    q = (x @ wq).reshape(B, S, XATTN_HEADS, XATTN_HD).astype(jnp.float32)
    s = jnp.einsum('bshd,bmhd->bhsm', q, mem_k) * (XATTN_HD ** -0.5)
    p = jax.nn.softmax(s, axis=-1)
    o = jnp.einsum('bhsm,bmhd->bshd', p, mem_v).reshape(B, S, D_MODEL).astype(x.dtype)
    return o @ wo


def sq_relu_mlp(x, w1, b1, w2, b2):
    return jnp.square(jax.nn.relu(x @ w1 + b1)) @ w2 + b2


def setup_inputs(seed: int = 0) -> dict:
    key = jax.random.key(seed)
    ks = jax.random.split(key, 20)
    H = MLSTM_HEADS
    nqk, nv = 2 * H * MLSTM_DK, H * MLSTM_DV
    p_in = nqk + nv + D_MODEL + 2 * H
    nrm = jax.random.normal
    f_bias = jnp.linspace(3.0, 6.0, H, dtype=jnp.float32)
    gate_b = jnp.concatenate([0.1 * nrm(ks[3], (N_A, H)),
                              f_bias[None, :] + 0.1 * nrm(ks[4], (N_A, H))], axis=-1)
    return {
        "x": nrm(ks[0], (BATCH, SEQ, D_MODEL), jnp.float32),
        "mem": nrm(ks[1], (BATCH, MEM_LEN, D_MODEL), jnp.float32),
        "mlstm_w_in": nrm(ks[2], (N_A, D_MODEL, p_in), jnp.float32) * D_MODEL ** -0.5,
        "mlstm_gate_b": gate_b,
        "mlstm_conv_w": nrm(ks[5], (N_A, CONV_W, nqk), jnp.float32) * CONV_W ** -0.5,
        "mlstm_norm_g": 1.0 + 0.02 * nrm(ks[6], (N_A, nv), jnp.float32),
        "mlstm_w_out": nrm(ks[7], (N_A, nv, D_MODEL), jnp.float32) * (nv ** -0.5) * BETA,
        "pool_w": nrm(ks[8], (N_B, POOL_GROUPS, POOL_GW, POOL_GW), jnp.float32) * (POOL_GW ** -0.5) * BETA,
        "pool_scale": 1.0 + 0.02 * nrm(ks[9], (N_B, D_MODEL), jnp.float32),
        "mem_w_kv": nrm(ks[10], (D_MODEL, 2 * D_MODEL), jnp.float32) * D_MODEL ** -0.5,
        "xattn_wq": nrm(ks[11], (DEPTH, D_MODEL, D_MODEL), jnp.float32) * D_MODEL ** -0.5,
        "xattn_wo": nrm(ks[12], (DEPTH, D_MODEL, D_MODEL), jnp.float32) * (D_MODEL ** -0.5) * BETA,
        "mlp_w1": nrm(ks[13], (DEPTH, D_MODEL, D_FF), jnp.float32) * D_MODEL ** -0.5,
        "mlp_b1": 0.01 * nrm(ks[14], (DEPTH, D_FF), jnp.float32),
        "mlp_w2": nrm(ks[15], (DEPTH, D_FF, D_MODEL), jnp.float32) * (D_FF ** -0.5) * BETA,
        "mlp_b2": 0.01 * nrm(ks[16], (DEPTH, D_MODEL), jnp.float32),
        "ln_g": 1.0 + 0.02 * nrm(ks[17], (DEPTH, 3, D_MODEL), jnp.float32),
        "ln_b": 0.02 * nrm(ks[18], (DEPTH, 3, D_MODEL), jnp.float32),
    }


def reference(x, mem, mlstm_w_in, mlstm_gate_b, mlstm_conv_w, mlstm_norm_g, mlstm_w_out,
              pool_w, pool_scale, mem_w_kv, xattn_wq, xattn_wo,
              mlp_w1, mlp_b1, mlp_w2, mlp_b2, ln_g, ln_b):
    B = mem.shape[0]
    kv = (mem @ mem_w_kv).astype(jnp.float32)
    mem_k = kv[..., :D_MODEL].reshape(B, MEM_LEN, XATTN_HEADS, XATTN_HD)
    mem_v = kv[..., D_MODEL:].reshape(B, MEM_LEN, XATTN_HEADS, XATTN_HD)
    for i in range(DEPTH):
        j = i // N_MIXERS
        if i % N_MIXERS == 0:
            y = mlstm_mixer(x, mlstm_w_in[j], mlstm_gate_b[j], mlstm_conv_w[j],
                            mlstm_norm_g[j], mlstm_w_out[j])
        else:
            y = pool_mixer(x, pool_w[j], pool_scale[j])
        x = layer_norm(ALPHA * x + y, ln_g[i, 0], ln_b[i, 0])
        x = layer_norm(ALPHA * x + memory_cross_attn(x, mem_k, mem_v, xattn_wq[i], xattn_wo[i]),
                       ln_g[i, 1], ln_b[i, 1])
        x = layer_norm(ALPHA * x + sq_relu_mlp(x, mlp_w1[i], mlp_b1[i], mlp_w2[i], mlp_b2[i]),
                       ln_g[i, 2], ln_b[i, 2])
    return x
```

```python
import numpy as np
import ml_dtypes
from contextlib import ExitStack
import concourse.bass as bass
import concourse.mybir as mybir
from concourse.bass_utils import run_bass_kernel_spmd

F32 = mybir.dt.float32
BF16 = mybir.dt.bfloat16
ALU = mybir.AluOpType
AF = mybir.ActivationFunctionType
AX = mybir.AxisListType

D = 1024
H = 4
DK = 128
DV = 256
DFF = 4096
MEM = 256
SEQ = 8192
BATCH = 4
ALPHA = 4.0 ** 0.25
EPS = 1e-5
import os as _os
EXCL = bool(_os.environ.get('EXCL'))
NSLOT = 6
NDS = 12
TOK_MAIN = 4096
MINI = 128
PRE = SEQ // 2 - MINI


class Buf:
    __slots__ = ("name", "w", "r", "excl")

    def __init__(self, name, excl=False):
        self.name = name
        self.w = None
        self.r = {}
        self.excl = excl


class Trk:
    def __init__(self, nc, es):
        self.nc = nc
        self.eng = {"pe": nc.tensor, "act": nc.scalar, "dve": nc.vector, "pool": nc.gpsimd, "sp": nc.sync}
        self.sem = {k: es.enter_context(nc.semaphore("s_" + k)) for k in self.eng}
        self.cnt = {k: 0 for k in self.eng}
        self.seen = {k: {} for k in self.eng}
        self.dsem = [es.enter_context(nc.semaphore("d%d" % i)) for i in range(NDS)]
        self.dcnt = [0] * NDS
        self.dnext = 0
        self.nwait = 0
        self.csem = []
        self.es = es

    def _semof(self, key):
        if isinstance(key, str):
            return self.sem[key]
        return self.dsem[key[1]] if key[0] == "d" else self.csem[key[1]]

    def dma_once(self, out, in_, q="pool"):
        i = len(self.csem)
        self.csem.append(self.es.enter_context(self.nc.semaphore("c%d" % i)))
        self.eng[q].dma_start(out=out, in_=in_).then_inc(self.csem[i], 16)
        return (("c", i), 16)

    def _needs(self, e, rd, wr):
        needs = {}

        def need(k, v):
            if v > needs.get(k, 0):
                needs[k] = v

        for b in rd:
            if b.w is not None:
                need(*b.w)
            if b.excl and EXCL:
                for k, v in b.r.items():
                    if k != e:
                        need(k, v)
        for b in wr:
            if b.w is not None and b.w[0] != e:
                need(*b.w)
            for k, v in b.r.items():
                if k != e:
                    need(k, v)
        return needs

    def _wait(self, e, needs):
        for key, val in needs.items():
            if key == e and (e == "pe" or val > self.cnt[e]):
                continue
            if self.seen[e].get(key, 0) >= val:
                continue
            self.eng[e].wait_ge(self._semof(key), val)
            self.seen[e][key] = val
            self.nwait += 1

    def op(self, e, fn, rd=(), wr=(), inc=True):
        self._wait(e, self._needs(e, rd, wr))
        ins = fn(self.eng[e])
        if inc:
            self.cnt[e] += 1
            ins.then_inc(self.sem[e], 1)
            st = self.cnt[e]
        else:
            st = self.cnt[e] + 1
        for b in rd:
            if st > b.r.get(e, 0):
                b.r[e] = st
        for b in wr:
            b.w = (e, st)
            b.r = {}
        return ins

    def dma(self, out, in_, rd=(), wr=(), q="sp", extra=()):
        i = self.dnext
        self.dnext = (i + 1) % NDS
        needs = self._needs(q, rd, wr)
        for k, v in extra:
            needs[k] = max(needs.get(k, 0), v)
        key = ("d", i)
        if self.dcnt[i] > 0:
            needs[key] = max(needs.get(key, 0), 16 * self.dcnt[i])
        self._wait(q, needs)
        self.dcnt[i] += 1
        self.eng[q].dma_start(out=out, in_=in_).then_inc(self.dsem[i], 16)
        st = 16 * self.dcnt[i]
        for b in rd:
            b.r[key] = st
        for b in wr:
            b.w = (key, st)
            b.r = {}
        return (key, st)


def build(cfg):
    n_pre = cfg.get("n_pre", PRE)
    n_main = cfg.get("n_main", TOK_MAIN)
    dbg = cfg.get("dbg", None)
    assert n_pre % 128 == 0 and n_main % 512 == 0

    nc = bass.Bass("TRN2", target_bir_lowering=False)
    es = ExitStack()

    def din(name, shape, dt=F32):
        return nc.dram_tensor(name, list(shape), dt, kind="ExternalInput").ap()

    xm = din("xm", [MINI + n_main, D])
    xp = din("xp", [max(n_pre, 128), D])
    memb = din("memb", [MEM, D])
    flag = din("flag", [128, 1])
    invc = din("invc", [128, 4, 16])
    w_in = din("w_in", [D, 3080])
    w_out = din("w_out", [D, D])
    w_kv = din("w_kv", [D, 2 * D])
    wq = din("wq", [2, D, D])
    wo = din("wo", [2, D, D])
    w1 = din("w1", [2, D, DFF])
    w2 = din("w2", [2, DFF, D])
    pool_w = din("pool_w", [4, 256, 256])
    pool_scale = din("pool_scale", [1, D])
    convw_d = din("convw", [128, 8, 4])
    normg_d = din("normg", [128, 8])
    lng_col_d = din("lng_col", [128, 6, 8])
    lnb_col_d = din("lnb_col", [128, 6, 8])
    lng_rows = din("lng_rows", [6, D])
    lnb_rows = din("lnb_rows", [6, D])
    b1_col_d = din("b1_col", [128, 2, 32])
    gateb_d = din("gateb", [4, 2])
    rows8_d = din("rows8", [128, D])
    sel_d = din("sel", [128, 6, 128])
    ident_d = din("ident", [128, 128])
    mask_d = din("mask", [128, 4, 128])
    out = nc.dram_tensor("out", [n_main, D], F32, kind="ExternalOutput").ap()

    uids = ["qk0", "qk1", "v0", "v1", "o0", "o1", "wout0", "wout1", "kv0", "kv1", "kv2", "kv3"]
    for l in range(2):
        uids += [("wq", l, 0), ("wq", l, 1), ("wo", l, 0), ("wo", l, 1)]
        uids += [("w1", l, i) for i in range(8)]
        uids += [("w2", l, n, q) for n in range(2) for q in range(4)]
    uidx = {u: i for i, u in enumerate(uids)}
    wsc = nc.dram_tensor("wsc", [len(uids), 128, 8, 512], BF16, kind="Internal").ap()

    T = Trk(nc, es)

    def sb(name, shape, dt=F32):
        return es.enter_context(nc.sbuf_tensor("sb_" + name, list(shape), dt))

    ring = [sb("ring%d" % i, [128, 8, 512], BF16) for i in range(NSLOT)]
    ringb = [Buf("ring%d" % i) for i in range(NSLOT)]
    R = sb("R", [128, 4, D])
    Rb = [Buf("R%d" % i) for i in range(4)]
    xT = sb("xT", [128, 8, 512], BF16)
    xTb = [[Buf("xTe%d" % i), Buf("xTo%d" % i)] for i in range(4)]
    FA = sb("FA", [128, 8, 512], BF16)
    FAb = [Buf("FA%d" % i) for i in range(8)]
    FB = sb("FB", [128, 8, 512], BF16)
    FBb = [Buf("FB%d" % i) for i in range(8)]
    SG = [sb("SG%d" % i, [128, D]) for i in range(3)]
    SGb = [Buf("SG%d" % i) for i in range(3)]
    sgn = [0]
    hT = sb("hT", [128, 32, 512], BF16)
    hTb = [Buf("hT%d" % i) for i in range(32)]
    hflat = hT[:].rearrange("p a b -> p (a b)")

    def carve(kb0, nbytes, shape, dt):
        n_el = nbytes // 2
        ap = hflat[:, kb0 * 512: kb0 * 512 + n_el]
        if dt == F32:
            ap = ap.bitcast(F32)
        if len(shape) == 1:
            ap = ap.rearrange("p (a b) -> p a b", b=shape[0])
        elif len(shape) == 2:
            ap = ap.rearrange("p (a b c) -> p a b c", b=shape[0], c=shape[1])
        return ap

    def bufs2(name, n=2):
        return [Buf("%s%d" % (name, i)) for i in range(n)]

    vw_ap, vw_b = carve(0, 4096, (4, 256), BF16), bufs2("vw")
    sgo_ap, sgo_b = carve(4, 4096, (1024,), BF16), bufs2("sgo")
    raw_ap, raw_b = carve(8, 2 * 516 * 4, (516,), F32), bufs2("raw")
    acc_ap, acc_b = carve(13, 4096, (512,), F32), bufs2("acc")
    ypre_ap, ypre_b = carve(17, 4096, (1024,), BF16), bufs2("ypre")
    ktok_ap, ktok_b = carve(21, 2048, (4, 128), BF16), bufs2("ktok")
    smt_ap, smt_b = carve(23, 2048, (4, 128), BF16), bufs2("smt")
    qat_ap, qat_b = carve(25, 2048, (4, 128), BF16), bufs2("qat")
    rs_ap, rs_b = carve(0, 8192, (512,), F32), bufs2("rs", 4)
    pt_ap, pt_b = carve(8, 8192, (2, 512), BF16), bufs2("pt", 4)
    arena = vw_b + sgo_b + raw_b + acc_b + ypre_b + ktok_b + smt_b + qat_b + rs_b + pt_b + hTb
    arena_owner = [None]

    def arena_switch(owner):
        if arena_owner[0] == owner:
            return
        arena_owner[0] = owner
        merged = {}
        for b in arena:
            if b.w is not None and b.w[1] > merged.get(b.w[0], 0):
                merged[b.w[0]] = b.w[1]
            for k, v in b.r.items():
                if v > merged.get(k, 0):
                    merged[k] = v
        for b in arena:
            b.w = None
            b.r = dict(merged)

    rtmp = sb("rtmp", [128, 2, 512])
    rtmpb = [Buf("rtmp0"), Buf("rtmp1")]

    def half(bufs, i):
        return [bufs[i]]

    def gt(name, n):
        return sb(name, [4, n]), Buf(name)

    gI, gIb = gt("gI", 512)
    gE, gEb = gt("gE", 512)
    gCS, gCSb = gt("gCS", 512)
    gU, gUb = gt("gU", 512)
    gW, gWb = gt("gW", 512)
    gFL, gFLb = gt("gFL", 512)
    gsm = sb("gsm", [4, 64])
    gsmb = Buf("gsm")
    ones4 = sb("ones4", [4, 128])
    ones4b = Buf("ones4")
    wfl = sb("wfl", [128, 4, 8])
    wflb = Buf("wfl")
    wcol = sb("wcol", [128, 4, 4], BF16)
    wcolb = Buf("wcol")
    arep = sb("arep", [128, 2, 16])
    arepb = Buf("arep")
    Cst = sb("Cst", [128, 4, 256])
    Cb_ = sb("Cbf", [128, 4, 256], BF16)
    nst = sb("nst", [128, 4])
    nbf = sb("nbf", [128, 4], BF16)
    Cstb, Cbb, nstb, nbfb = Buf("Cst"), Buf("Cbf"), Buf("nst"), Buf("nbf")
    halo = sb("halo", [128, 8, 3])
    halob = Buf("halo")
    small = sb("small", [128, 2, 32])
    smallb = [Buf("small0"), Buf("small1")]
    smallb2 = [Buf("small0b"), Buf("small1b")]
    lnst = sb("lnst", [128, 4, 16])
    lnstb = [Buf("lnst%d" % i) for i in range(4)]
    junk = sb("junk", [128, 256])
    junkb = Buf("junk")
    x1T = sb("x1T", [128, 8, 528])
    x1b = [Buf("x1T%d" % i) for i in range(4)]
    ptmp = sb("ptmp", [128, 2, 2, 528])
    ptmpb = [Buf("ptmp0"), Buf("ptmp1")]
    ptiny = sb("ptiny", [128, 2, 16])
    ptinyb = Buf("ptiny")
    poolw = sb("poolw", [128, 4, 2, 256], BF16)
    poolwb = Buf("poolw")
    KT = sb("KT", [128, 8, 256], BF16)
    Vm = sb("Vm", [128, 2, 1024], BF16)
    KTb, Vmb = Buf("KT"), Buf("Vm")
    ident = sb("ident", [128, 128])
    identb_ = sb("identbf", [128, 128], BF16)
    maskt = sb("maskt", [128, 4, 128], BF16)
    onesb = sb("onesb", [128, 128], BF16)
    convw = sb("convw", [128, 8, 4])
    normg = sb("normgc", [128, 8])
    lngc = sb("lngc", [128, 6, 8])
    lnbc = sb("lnbc", [128, 6, 8])
    b1c = sb("b1c", [128, 2, 32])
    gateb = sb("gatebt", [4, 4])
    rows8 = sb("rows8", [128, D], BF16)
    sel = sb("selt", [128, 6, 128], BF16)
    wg = sb("wg", [128, 8, 8], BF16)
    flg = sb("flg", [128, 1])
    invct = sb("invct", [128, 4, 16])
    epsc = sb("epsc", [128, 8])
    cb = Buf("consts")

    PB = [es.enter_context(nc.psum_tensor("pb%d" % i, [128, 512], F32)) for i in range(8)]
    PBb = [Buf("pb%d" % i, excl=True) for i in range(8)]
    pb5lock = Buf("pb5lock")
    pfn = [0]
    pf_list = [[6, 7]]

    def next_pf():
        l = pf_list[0]
        i = l[pfn[0] % len(l)]
        pfn[0] += 1
        return i

    def pbf(i):
        return PB[i][:].bitcast(BF16)

    dumps = []

    def dump(name, ap, bufs, shape, dt=F32):
        if dbg is None or name not in dbg:
            return
        t = nc.dram_tensor("dbg_" + name, list(shape), dt, kind="ExternalOutput").ap()
        dumps.append(T.dma(t, ap, rd=bufs))

    stage = R[:, 0, :]

    def ld(dst, src, b=cb):
        return T.dma(dst, src, wr=[b])

    ld(ident[:], ident_d[:, :])
    ld(convw[:], convw_d[:, :, :])
    ld(normg[:], normg_d[:, :])
    ld(lngc[:], lng_col_d[:, :, :])
    ld(lnbc[:], lnb_col_d[:, :, :])
    ld(b1c[:], b1_col_d[:, :, :])
    ld(gateb[:, 0:2], gateb_d[:, :])
    ld(flg[:], flag[:, :])
    ld(invct[:], invc[:, :, :])
    T.dma(R[:, 0, 0:512], mask_d.rearrange("p a b -> p (a b)"), wr=[Rb[0]])
    T.dma(R[:, 1, :], rows8_d[:, :], wr=[Rb[1]])
    T.dma(R[:, 2, 0:768], sel_d.rearrange("p a b -> p (a b)"), wr=[Rb[2]])
    T.dma(R[:, 3, 0:64].rearrange("p (a b) -> p a b", b=8), w_in[:, 3072:3080].rearrange("(kc p) g -> p kc g", p=128), wr=[Rb[3]])
    T.op("dve", lambda e: e.tensor_copy(out=maskt[:].rearrange("p a b -> p (a b)"), in_=R[:, 0, 0:512]), rd=[Rb[0]], wr=[cb])
    T.op("dve", lambda e: e.tensor_copy(out=rows8[:], in_=R[:, 1, :]), rd=[Rb[1]], wr=[cb])
    T.op("dve", lambda e: e.tensor_copy(out=sel[:].rearrange("p a b -> p (a b)"), in_=R[:, 2, 0:768]), rd=[Rb[2]], wr=[cb])
    T.op("dve", lambda e: e.tensor_copy(out=wg[:].rearrange("p a b -> p (a b)"), in_=R[:, 3, 0:64]), rd=[Rb[3]], wr=[cb])
    T.op("dve", lambda e: e.tensor_copy(out=identb_[:], in_=ident[:]), rd=[cb], wr=[cb])
    T.op("pool", lambda e: e.memset(onesb[:], 1.0), wr=[cb])
    T.op("pool", lambda e: e.memset(ones4[:], 1.0), wr=[ones4b])
    T.op("pool", lambda e: e.memset(epsc[:, 0:1], EPS), wr=[cb])
    T.op("pool", lambda e: e.memset(epsc[:, 1:2], 1.0), wr=[cb])
    T.op("pool", lambda e: e.memset(epsc[:, 2:3], float(np.log(DK ** -0.5))), wr=[cb])
    T.op("pool", lambda e: e.memset(epsc[:, 4:8], -0.5), wr=[cb])
    T.op("dve", lambda e: e.tensor_scalar(out=gateb[:, 2:3], in0=gateb[:, 1:2], scalar1=-1.0, scalar2=None, op0=ALU.mult), rd=[cb], wr=[cb])
    T.op("pool", lambda e: e.memset(Cst[:], 0.0), wr=[Cstb])
    T.op("pool", lambda e: e.memset(Cb_[:], 0.0), wr=[Cbb])
    T.op("pool", lambda e: e.memset(nst[:], 0.0), wr=[nstb])
    T.op("pool", lambda e: e.memset(nbf[:], 0.0), wr=[nbfb])
    T.op("pool", lambda e: e.memset(halo[:], 0.0), wr=[halob])
    T.op("pool", lambda e: e.memset(gsm[:], 0.0), wr=[gsmb])
    T.op("pool", lambda e: e.memset(x1T[:, :, 0:16], 0.0), wr=x1b)

    cast_st = {}
    cast_jobs = []

    def cast_units(src2d, u0, nu):
        for n in range(nu):
            cast_jobs.append((u0 + n, src2d[:, n * 512:(n + 1) * 512].rearrange("(kc p) j -> p kc j", p=128)))

    cast_units(w_kv, uidx["kv0"], 4)
    cast_units(w_in[:, 512:1024], uidx["qk1"], 1)
    cast_units(w_in[:, 1024:2048], uidx["v0"], 2)
    cast_units(w_in[:, 0:512], uidx["qk0"], 1)
    cast_units(w_in[:, 2048:3072], uidx["o0"], 2)
    cast_units(w_out, uidx["wout0"], 2)
    for l in range(2):
        cast_units(wq[l], uidx[("wq", l, 0)], 2)
        cast_units(wo[l], uidx[("wo", l, 0)], 2)
        cast_units(w1[l], uidx[("w1", l, 0)], 8)
        for n in range(2):
            for q in range(4):
                cast_jobs.append((uidx[("w2", l, n, q)], w2[l][q * 1024:(q + 1) * 1024, n * 512:(n + 1) * 512].rearrange("(kc p) j -> p kc j", p=128)))

    def emit_casts(k):
        for _ in range(k):
            if cast_jobs:
                u, src = cast_jobs.pop(0)
                cast_st[u] = T.dma_once(wsc[u], src, q="pool")

    emit_casts(7)

    class WS:
        def __init__(self):
            self.seq = []
            self.pos_load = 0
            self.pos_use = 0
            self.released = 0

        def plan(self, lst):
            self.seq += lst

        def _load(self):
            j = self.pos_load
            if j >= len(self.seq):
                return
            slot = j % NSLOT
            u = uidx[self.seq[j]]
            while u not in cast_st:
                emit_casts(1)
            T.dma(ring[slot][:], wsc[u], wr=[ringb[slot]], extra=[cast_st[u]])
            self.pos_load += 1

        def start(self):
            while self.pos_load < min(NSLOT, len(self.seq)):
                self._load()

        def get(self, uid):
            assert self.seq[self.pos_use] == uid, (self.seq[self.pos_use], uid)
            assert self.pos_use < self.pos_load, "unit not loaded (ring too small for this use pattern)"
            slot = self.pos_use % NSLOT
            self.pos_use += 1
            return ring[slot], ringb[slot]

        def release(self, k=1):
            for _ in range(k):
                self.released += 1
                assert self.released <= self.pos_use
                if self.pos_load < self.released + NSLOT:
                    self._load()

    ws = WS()

    class StageR:
        def sub(self, s, lo, hi):
            return R[:, s, lo:hi]

        def bufs(self, s):
            return [Rb[s]]

        def load(self, src, tok0, nsub):
            T.dma(R[:, 0:nsub, :], src[tok0:tok0 + 128 * nsub, :].rearrange("(s p) d -> p s d", p=128), wr=Rb[0:nsub])

    class StageX:
        def sub(self, s, lo, hi):
            n = lo // 512
            assert (hi - 1) // 512 == n
            return x1T[:, 2 * s + n, 16 + lo - n * 512: 16 + hi - n * 512]

        def bufs(self, s):
            return [x1b[s]]

        def load(self, src, tok0, nsub):
            for s_ in range(nsub):
                T.dma(x1T[:, 2 * s_:2 * s_ + 2, 16:528], src[tok0 + 128 * s_:tok0 + 128 * (s_ + 1), :].rearrange("p (n j) -> p n j", j=512), wr=[x1b[s_]])

    STR, STX = StageR(), StageX()

    def feat_major(s, dst, dstbufs, q=None, f32dst=False, stage=None):
        stage = stage or STR
        pv = PB[4][:].rearrange("p (a b) -> p a b", b=128)
        pv2 = PB[5][:].rearrange("p (a b) -> p a b", b=128)
        for kc in range(8):
            o = (pv if kc < 4 else pv2)[:, kc % 4, :]
            T.op("pe", lambda e, o=o, kc=kc: e.transpose(out=o, in_=stage.sub(s, kc * 128, (kc + 1) * 128), identity=ident[:]),
                 rd=stage.bufs(s) + [cb], wr=[PBb[4 if kc < 4 else 5]], inc=(kc % 4 == 3))
        off = 16 if f32dst else 0
        for kc in range(8):
            src = (pv if kc < 4 else pv2)[:, kc % 4, :]
            d = dst[:, kc, off + s * 128: off + (s + 1) * 128]
            pb = PBb[4 if kc < 4 else 5]
            if q is None:
                if kc < 4:
                    T.op("dve", lambda e, d=d, src=src: e.tensor_copy(out=d, in_=src), rd=[pb], wr=dstbufs(kc))
                else:
                    T.op("act", lambda e, d=d, src=src: e.activation(out=d, in_=src, func=AF.Identity), rd=[pb], wr=dstbufs(kc))
            else:
                g = lngc[:, q, kc:kc + 1]
                b = lnbc[:, q, kc:kc + 1]
                if kc < 4:
                    T.op("dve", lambda e, d=d, src=src, g=g, b=b: e.tensor_scalar(out=d, in0=src, scalar1=g, scalar2=b, op0=ALU.mult, op1=ALU.add),
                         rd=[pb, cb], wr=dstbufs(kc))
                else:
                    T.op("act", lambda e, d=d, src=src, g=g, b=b: e.activation(out=d, in_=src, func=AF.Identity, bias=b, scale=g),
                         rd=[pb, cb], wr=dstbufs(kc))

    FXb = [[Buf("FXe%d" % i), Buf("FXo%d" % i)] for i in range(4)]
    XTA = {"ap": xT, "b": xTb}
    XTB = {"ap": FB, "b": FXb}

    def xr(X, subs):
        out_ = []
        for s_ in subs:
            out_ += X["b"][s_]
        return out_

    def xT_bufs(s, X=None):
        X = X or XTA
        return lambda kc: [X["b"][s][kc // 4]]

    def alias_switch(old, new):
        merged = {}
        for b in old:
            if b.w is not None and b.w[1] > merged.get(b.w[0], 0):
                merged[b.w[0]] = b.w[1]
            for k, v in b.r.items():
                if v > merged.get(k, 0):
                    merged[k] = v
        for b in new:
            for k, v in merged.items():
                if v > b.r.get(k, 0):
                    b.r[k] = v

    def sg_load(row):
        i = sgn[0] % 3
        sgn[0] += 1
        T.dma(SG[i][:], row.partition_broadcast(128), wr=[SGb[i]])
        return i

    def ln_stage(nsub, proj, sg_row, selq, q, dst=None, dstbufs=None, f32dst=False, final=False, res=None):
        sgi = sg_load(sg_row) if sg_row is not None else None
        if final:
            gi = sg_load(lng_rows[5:6, :])
            bi = sg_load(lnb_rows[5:6, :])
        def post(s):
            if final:
                T.op("pool", lambda e: e.tensor_tensor(out=R[:, s, :], in0=R[:, s, :], in1=SG[gi][:], op=ALU.mult), rd=[Rb[s], SGb[gi]], wr=[Rb[s]])
                T.op("dve", lambda e: e.tensor_tensor(out=R[:, s, :], in0=R[:, s, :], in1=SG[bi][:], op=ALU.add), rd=[Rb[s], SGb[bi]], wr=[Rb[s]])
            else:
                feat_major(s, dst, dstbufs(s), q=q, f32dst=f32dst)

        for n in range(2):
            cs = slice(n * 512, (n + 1) * 512)
            for s in range(nsub):
                bk = s
                first = True
                if selq is not None:
                    T.op("pe", lambda e, bk=bk: e.matmul(PB[bk][:], lhsT=sel[:, q, :], rhs=rows8[:, cs], start=True, stop=False),
                         rd=[cb], wr=[PBb[bk]], inc=False)
                    first = False
                proj(s, n, bk, first)
                if sgi is not None:
                    T.op("pool", lambda e: e.tensor_tensor(out=R[:, s, cs], in0=R[:, s, cs], in1=SG[sgi][:, cs], op=ALU.mult),
                         rd=[Rb[s], SGb[sgi]], wr=[Rb[s]])
                rsrc = R[:, s, cs] if res is None else res.sub(s, n * 512, (n + 1) * 512)
                rbufs = [Rb[s]] if res is None else res.bufs(s)
                T.op("dve", lambda e, bk=bk: e.scalar_tensor_tensor(out=R[:, s, cs], in0=rsrc, scalar=ALPHA, in1=PB[bk][:], op0=ALU.mult, op1=ALU.add),
                     rd=rbufs + [PBb[bk]], wr=[Rb[s]])
                st = lnst[:, s, :]
                sbk = lnstb[s]
                if n == 0:
                    T.op("dve", lambda e: e.bn_stats(out=st[:, 0:6], in_=R[:, s, 0:512]), rd=[Rb[s]], wr=[sbk])
                if n == 1:
                    T.op("dve", lambda e: e.bn_stats(out=st[:, 6:12], in_=R[:, s, 512:1024]), rd=[Rb[s]], wr=[sbk])
                    T.op("dve", lambda e: e.bn_aggr(out=st[:, 12:14], in_=st[:, 0:12]), rd=[sbk], wr=[sbk])
                    T.op("act", lambda e: e.activation(out=st[:, 14:15], in_=st[:, 13:14], func=AF.Sqrt, bias=epsc[:, 0:1], scale=1.0), rd=[sbk, cb], wr=[sbk])
                    T.op("dve", lambda e: e.reciprocal(out=st[:, 14:15], in_=st[:, 14:15]), rd=[sbk], wr=[sbk])
                    T.op("dve", lambda e: e.tensor_scalar(out=st[:, 15:16], in0=st[:, 12:13], scalar1=-1.0, scalar2=st[:, 14:15], op0=ALU.mult, op1=ALU.mult),
                         rd=[sbk], wr=[sbk])
                    T.op("act", lambda e: e.activation(out=R[:, s, :], in_=R[:, s, :], func=AF.Identity, bias=st[:, 15:16], scale=st[:, 14:15]),
                         rd=[Rb[s], sbk], wr=[Rb[s]])
                    if s > 0:
                        post(s - 1)
        post(nsub - 1)

    def tok_proj(srcT, srcbufs, units, kcn=8):
        def proj(s, n, bk, first):
            w, wb = units[n]
            for kc in range(kcn):
                T.op("pe", lambda e, kc=kc: e.matmul(PB[bk][:], lhsT=srcT[:, kc, s * 128:(s + 1) * 128], rhs=w[:, kc, :],
                                                      start=(first and kc == 0), stop=(kc == kcn - 1)),
                     rd=[wb] + srcbufs(s, kc), wr=[PBb[bk]], inc=(kc == kcn - 1))
        return proj

    def gates(nsub, X):
        Tn = 128 * nsub
        xT = X['ap']
        for gi_, (bk, c0) in enumerate(((6, 0), (7, 4))):
            for kc in range(8):
                T.op("pe", lambda e, kc=kc: e.matmul(PB[bk][0:4, 0:Tn], lhsT=wg[:, kc, c0:c0 + 4], rhs=xT[:, kc, 0:Tn], start=(kc == 0), stop=(kc == 7)),
                     rd=[cb] + xr(X, range(nsub)), wr=[PBb[bk]], inc=(kc == 7))
        yield
        T.op("act", lambda e: e.activation(out=gI[:, 0:Tn], in_=PB[6][0:4, 0:Tn], func=AF.Identity, bias=gateb[:, 0:1], scale=1.0), rd=[PBb[6], cb], wr=[gIb])
        T.op("act", lambda e: e.activation(out=gE[:, 0:Tn], in_=PB[7][0:4, 0:Tn], func=AF.Exp, bias=gateb[:, 2:3], scale=-1.0), rd=[PBb[7], cb], wr=[gEb])
        T.op("act", lambda e: e.activation(out=gE[:, 0:Tn], in_=gE[:, 0:Tn], func=AF.Ln, bias=epsc[0:4, 1:2], scale=1.0), rd=[gEb, cb], wr=[gEb])
        yield
        for c in range(nsub):
            T.op("dve", lambda e, c=c: e.tensor_tensor_scan(out=gCS[:, c * 128:(c + 1) * 128], data0=ones4[:, :], data1=gE[:, c * 128:(c + 1) * 128],
                                                            initial=0.0, op0=ALU.mult, op1=ALU.add), rd=[gEb, ones4b], wr=[gCSb])
        T.op("dve", lambda e: e.tensor_tensor(out=gU[:, 0:Tn], in0=gI[:, 0:Tn], in1=gCS[:, 0:Tn], op=ALU.add), rd=[gIb, gCSb], wr=[gUb])
        T.op("dve", lambda e: e.tensor_reduce(out=gsm[:, 0:nsub], in_=gU[:, 0:Tn].rearrange("p (c t) -> p c t", t=128), axis=AX.X, op=ALU.max), rd=[gUb], wr=[gsmb])
        yield
        for c in range(nsub):
            T.op("dve", lambda e, c=c: e.tensor_tensor(out=gsm[:, 4 + c:5 + c], in0=gsm[:, 12:13], in1=gsm[:, c:c + 1], op=ALU.max), rd=[gsmb], wr=[gsmb])
            T.op("dve", lambda e, c=c: e.tensor_tensor(out=gsm[:, 8 + c:9 + c], in0=gsm[:, 12:13], in1=gsm[:, 4 + c:5 + c], op=ALU.subtract), rd=[gsmb], wr=[gsmb])
            T.op("dve", lambda e, c=c: e.tensor_tensor(out=gsm[:, 12:13], in0=gsm[:, 4 + c:5 + c], in1=gCS[:, c * 128 + 127:c * 128 + 128], op=ALU.subtract),
                 rd=[gsmb, gCSb], wr=[gsmb])
        yield
        mcb = gsm[:, 4:4 + nsub].unsqueeze(2).to_broadcast([4, nsub, 128])
        T.op("dve", lambda e: e.tensor_tensor(out=gW[:, 0:Tn].rearrange("p (c t) -> p c t", t=128), in0=gU[:, 0:Tn].rearrange("p (c t) -> p c t", t=128),
                                              in1=mcb, op=ALU.subtract), rd=[gUb, gsmb], wr=[gWb])
        T.op("dve", lambda e: e.tensor_tensor(out=gFL[:, 0:Tn].rearrange("p (c t) -> p c t", t=128), in0=gCS[:, 0:Tn].rearrange("p (c t) -> p c t", t=128),
                                              in1=mcb, op=ALU.subtract), rd=[gCSb, gsmb], wr=[gFLb])
        T.op("act", lambda e: e.activation(out=gW[:, 0:Tn], in_=gW[:, 0:Tn], func=AF.Exp), rd=[gWb], wr=[gWb])
        T.op("act", lambda e: e.activation(out=gFL[:, 0:Tn], in_=gFL[:, 0:Tn], func=AF.Exp), rd=[gFLb], wr=[gFLb])
        yield
        T.op("dve", lambda e: e.tensor_tensor(out=gsm[:, 16:16 + 4 * nsub].rearrange("p (c h) -> p c h", h=4),
                                              in0=gsm[:, 8:8 + nsub].unsqueeze(2).to_broadcast([4, nsub, 4]),
                                              in1=ident[0:4, 0:4].unsqueeze(1).to_broadcast([4, nsub, 4]), op=ALU.mult), rd=[gsmb, cb], wr=[gsmb])
        pv = PB[5][:, 0:8 * nsub].rearrange("p (c j) -> p c j", j=8)
        for c in range(nsub):
            T.op("pe", lambda e, c=c: e.transpose(out=pv[:, c, 0:4], in_=gW[:, c * 128:(c + 1) * 128], identity=ident[0:4, 0:4]), rd=[gWb, cb], wr=[PBb[5]], inc=False)
            T.op("pe", lambda e, c=c: e.transpose(out=pv[:, c, 4:8], in_=gFL[:, c * 128:(c + 1) * 128], identity=ident[0:4, 0:4]), rd=[gFLb, cb], wr=[PBb[5]], inc=False)
        T.op("pe", lambda e: e.matmul(PB[5][:, 64:64 + 4 * nsub], lhsT=ones4[:, :], rhs=gsm[:, 16:16 + 4 * nsub], start=True, stop=True), rd=[ones4b, gsmb], wr=[PBb[5]])
        yield
        T.op("dve", lambda e: e.tensor_copy(out=wfl[:, 0:nsub, :], in_=pv), rd=[PBb[5]], wr=[wflb, pb5lock])
        T.op("act", lambda e: e.activation(out=wcol[:, 0:nsub, :], in_=pv[:, :, 0:4], func=AF.Identity), rd=[PBb[5]], wr=[wcolb, pb5lock])
        T.op("act", lambda e: e.activation(out=arep[:, 0, 0:4 * nsub], in_=PB[5][:, 64:64 + 4 * nsub], func=AF.Exp), rd=[PBb[5]], wr=[arepb, pb5lock])
        T.op("act", lambda e: e.activation(out=arep[:, 1, 0:4 * nsub], in_=PB[5][:, 64:64 + 4 * nsub], func=AF.Exp, bias=epsc[:, 2:3], scale=1.0), rd=[PBb[5], cb], wr=[arepb, pb5lock])

    def qk_conv(nsub, chunks, wunits, X, hook=None):
        Tn = 128 * nsub
        xT = X['ap']
        pend = None
        for c in chunks:
            if hook is not None:
                hook()
            w, wb = wunits[c // 4]
            col0 = (c % 4) * 128
            bk = next_pf()
            b = c % 2
            rawv = raw_ap[:, b, :]
            accv = acc_ap[:, b, 0:Tn]
            rb = half(raw_b, b)
            ab = half(acc_b, b)
            for kc in range(8):
                T.op("pe", lambda e, kc=kc: e.matmul(PB[bk][:, 0:Tn], lhsT=w[:, kc, col0:col0 + 128], rhs=xT[:, kc, 0:Tn], start=(kc == 0), stop=(kc == 7)),
                     rd=[wb] + xr(X, range(nsub)), wr=[PBb[bk]], inc=(kc == 7))
            T.op("pool", lambda e: e.tensor_copy(out=rawv[:, 0:3], in_=halo[:, c, :]), rd=[halob], wr=rb)
            T.op("act", lambda e: e.activation(out=rawv[:, 3:3 + Tn], in_=PB[bk][:, 0:Tn], func=AF.Identity), rd=[PBb[bk]], wr=rb)
            T.op("act", lambda e: e.activation(out=accv, in_=PB[bk][:, 0:Tn], func=AF.Identity, scale=convw[:, c, 3:4]), rd=[PBb[bk], cb], wr=ab)
            T.op("pool", lambda e: e.tensor_copy(out=halo[:, c, :], in_=rawv[:, Tn:Tn + 3]), rd=rb, wr=[halob])
            for j in (2, 1, 0):
                T.op("dve", lambda e, j=j: e.scalar_tensor_tensor(out=accv, in0=rawv[:, j:j + Tn], scalar=convw[:, c, j:j + 1], in1=accv, op0=ALU.mult, op1=ALU.add),
                     rd=rb + ab + [cb], wr=ab)
            if pend is not None:
                pc, pacc, pab = pend
                T.op("act", lambda e: e.activation(out=FA[:, pc, 0:Tn], in_=pacc, func=AF.Silu), rd=pab, wr=[FAb[pc]])
            pend = (c, accv, ab)
        pc, pacc, pab = pend
        T.op("act", lambda e: e.activation(out=FA[:, pc, 0:Tn], in_=pacc, func=AF.Silu), rd=pab, wr=[FAb[pc]])

    def mlstm_front(nsub, state_only, X):
        arena_switch('mlstm')
        pf_list[0] = [6, 7]
        gg = gates(nsub, X)
        next(gg)
        hook = lambda: next(gg, None)
        if state_only:
            qk1 = ws.get("qk1")
            qk_conv(nsub, range(4, 8), {1: qk1}, X, hook)
            ws.release(1)
        else:
            qk0 = ws.get("qk0")
            qk1 = ws.get("qk1")
            qk_conv(nsub, range(0, 8), {0: qk0, 1: qk1}, X, hook)
            ws.release(2)
        for _ in gg:
            pass

    def mlstm_loop(nsub, state_only, X):
        Tn = 128 * nsub
        xT = X['ap']
        vu = [ws.get("v0"), ws.get("v1")]
        ou = None if state_only else [ws.get("o0"), ws.get("o1")]
        numv = [PB[0][:].rearrange("p (h d) -> p h d", d=256), PB[1][:].rearrange("p (h d) -> p h d", d=256)]
        dcv = [PB[2][:].rearrange("p (h d) -> p h d", d=256), PB[3][:].rearrange("p (h d) -> p h d", d=256)]
        den = PB[5][:, 128:132]
        dn = PB[5][:, 136:140]
        s4 = PB[4][:].rearrange("p (h t) -> p h t", t=128)

        def emit_ypre_T(pc, pb_):
            ypv = ypre_ap[:, pb_, :]
            ypb = half(ypre_b, pb_)
            bk = next_pf()
            yv = pbf(bk)[:, 0:1024].rearrange("p (a t) -> p a t", t=128)
            for kc in range(8):
                T.op("pe", lambda e, kc=kc: e.transpose(out=yv[:, kc, :], in_=ypv[:, kc * 128:(kc + 1) * 128], identity=identb_[:]), rd=ypb + [cb], wr=[PBb[bk]], inc=(kc == 7))
            T.op("dve", lambda e: e.tensor_tensor(out=FB[:, :, pc * 128:(pc + 1) * 128], in0=yv, in1=normg[:, :].unsqueeze(2).to_broadcast([128, 8, 128]), op=ALU.mult),
                 rd=[PBb[bk], cb], wr=FBb)

        def front_part(c):
            b = c % 2
            ccols = slice(c * 128, (c + 1) * 128)
            if not state_only:
                smv = smt_ap[:, b]
                smb = half(smt_b, b)
                qav = qat_ap[:, b]
                qab = half(qat_b, b)
                for h in range(4):
                    T.op("pe", lambda e, h=h: e.matmul(s4[:, h, :], lhsT=FA[:, 4 + h, ccols], rhs=FA[:, h, ccols], start=True, stop=True),
                         rd=[FAb[4 + h], FAb[h]], wr=[PBb[4]], inc=(h == 3))
                T.op("dve", lambda e: e.scalar_tensor_tensor(out=smv, in0=s4, scalar=float(DK ** -0.5), in1=maskt[:], op0=ALU.mult, op1=ALU.mult),
                     rd=[PBb[4], cb], wr=smb)
                T.op("pool", lambda e: e.tensor_tensor(out=qav, in0=FA[:, 0:4, ccols], in1=arep[:, 1, 4 * c:4 * c + 4].unsqueeze(2).to_broadcast([128, 4, 128]), op=ALU.mult),
                     rd=FAb[0:4] + [arepb], wr=qab)
            bk = next_pf()
            kv_ = pbf(bk)[:, 0:512].rearrange("p (h d) -> p h d", d=128)
            for h in range(4):
                T.op("pe", lambda e, h=h: e.transpose(out=kv_[:, h, :], in_=FA[:, 4 + h, ccols], identity=identb_[:]), rd=[FAb[4 + h], cb], wr=[PBb[bk]], inc=(h == 3))
            T.op("act", lambda e: e.activation(out=ktok_ap[:, b], in_=kv_, func=AF.Identity), rd=[PBb[bk]], wr=half(ktok_b, b))

        def part_b(c):
            b = c % 2
            sm = small[:, b, :]
            smb_ = smallb[b]
            sgv = sgo_ap[:, b, :]
            sgb = half(sgo_b, b)
            T.op("dve", lambda e: e.tensor_tensor(out=sm[:, 12:16], in0=sm[:, 4:8], in1=sm[:, 4:8], op=ALU.mult), rd=[smb_], wr=[smb_])
            T.op("dve", lambda e: e.tensor_tensor(out=sm[:, 12:16], in0=sm[:, 12:16], in1=sm[:, 8:12], op=ALU.mult), rd=[smb_, smallb2[b]], wr=[smb_])
            T.op("dve", lambda e: e.tensor_scalar(out=sm[:, 12:16], in0=sm[:, 12:16], scalar1=1.0 / DV, scalar2=EPS, op0=ALU.mult, op1=ALU.add), rd=[smb_], wr=[smb_])
            T.op("act", lambda e: e.activation(out=sm[:, 12:16], in_=sm[:, 12:16], func=AF.Sqrt), rd=[smb_], wr=[smb_])
            T.op("dve", lambda e: e.reciprocal(out=sm[:, 16:20], in_=sm[:, 12:16]), rd=[smb_], wr=[smb_])
            T.op("dve", lambda e: e.tensor_tensor(out=sm[:, 20:24], in0=sm[:, 16:20], in1=sm[:, 4:8], op=ALU.mult), rd=[smb_], wr=[smb_])
            ypv = ypre_ap[:, b, :]
            ypb = half(ypre_b, b)
            for h in range(4):
                T.op("dve", lambda e, h=h: e.scalar_tensor_tensor(out=ypv[:, h * 256:(h + 1) * 256], in0=numv[h // 2][:, h % 2, :], scalar=sm[:, 20 + h:21 + h],
                                                                  in1=sgv[:, h * 256:(h + 1) * 256], op0=ALU.mult, op1=ALU.mult),
                     rd=[PBb[h // 2], smb_] + sgb, wr=ypb)
            return (c, b)

        pend = None
        pendb = None
        front_part(0)
        for c in range(nsub):
            b = c % 2
            ccols = slice(c * 128, (c + 1) * 128)
            vwv = vw_ap[:, b]
            vwb = half(vw_b, b)
            ktv = ktok_ap[:, b]
            ktb = half(ktok_b, b)
            if not state_only:
                smv = smt_ap[:, b]
                smb = half(smt_b, b)
                qav = qat_ap[:, b]
                qab = half(qat_b, b)
                sgv = sgo_ap[:, b, :]
                sgb = half(sgo_b, b)
            for n in range(2):
                bk = next_pf()
                w, wb = vu[n]
                for kc in range(8):
                    T.op("pe", lambda e, kc=kc: e.matmul(PB[bk][:], lhsT=xT[:, kc, ccols], rhs=w[:, kc, :], start=(kc == 0), stop=(kc == 7)),
                         rd=[wb] + X['b'][c], wr=[PBb[bk]], inc=(kc == 7))
                T.op("dve", lambda e: e.tensor_tensor(out=vwv[:, 2 * n:2 * n + 2, :], in0=PB[bk][:].rearrange("p (h d) -> p h d", d=256),
                                                      in1=wfl[:, c, 2 * n:2 * n + 2].unsqueeze(2).to_broadcast([128, 2, 256]), op=ALU.mult),
                     rd=[PBb[bk], wflb], wr=vwb)
            if pendb is not None:
                pend = part_b(pendb)
                pendb = None
            if not state_only:
                for n in range(2):
                    bk = next_pf()
                    w, wb = ou[n]
                    for kc in range(8):
                        T.op("pe", lambda e, kc=kc: e.matmul(PB[bk][:], lhsT=xT[:, kc, ccols], rhs=w[:, kc, :], start=(kc == 0), stop=(kc == 7)),
                             rd=[wb] + X['b'][c], wr=[PBb[bk]], inc=(kc == 7))
                    T.op("act", lambda e: e.activation(out=sgv[:, n * 512:(n + 1) * 512], in_=PB[bk][:], func=AF.Sigmoid), rd=[PBb[bk]], wr=sgb)
            if pend is not None:
                emit_ypre_T(*pend)
                pend = None
            if not state_only:
                for h in range(4):
                    nb_ = PBb[h // 2]
                    o = numv[h // 2][:, h % 2, :]
                    T.op("pe", lambda e, h=h, o=o: e.matmul(o, lhsT=smv[:, h, :], rhs=vwv[:, h, :], start=True, stop=False), rd=smb + vwb, wr=[nb_], inc=False)
                    T.op("pe", lambda e, h=h, o=o: e.matmul(o, lhsT=qav[:, h, :], rhs=Cb_[:, h, :], start=False, stop=True), rd=qab + [Cbb], wr=[nb_], inc=False)
                    T.op("pe", lambda e, h=h: e.matmul(den[:, h:h + 1], lhsT=smv[:, h, :], rhs=wcol[:, c, h:h + 1], start=True, stop=False), rd=smb + [wcolb], wr=[PBb[5]], inc=False)
                    T.op("pe", lambda e, h=h: e.matmul(den[:, h:h + 1], lhsT=qav[:, h, :], rhs=nbf[:, h:h + 1], start=False, stop=True), rd=qab + [nbfb], wr=[PBb[5]], inc=(h == 3))
            for h in range(4):
                T.op("pe", lambda e, h=h: e.matmul(dcv[h // 2][:, h % 2, :], lhsT=ktv[:, h, :], rhs=vwv[:, h, :], start=True, stop=True), rd=ktb + vwb, wr=[PBb[2 + h // 2]], inc=False)
                T.op("pe", lambda e, h=h: e.matmul(dn[:, h:h + 1], lhsT=ktv[:, h, :], rhs=wcol[:, c, h:h + 1], start=True, stop=True), rd=ktb + [wcolb], wr=[PBb[5]], inc=(h == 3))
            if c + 1 < nsub:
                front_part(c + 1)
            if not state_only:
                sm = small[:, b, :]
                smb_ = smallb[b]
                T.op("dve", lambda e: e.tensor_scalar(out=sm[:, 24:28], in0=den, scalar1=-1.0, scalar2=None, op0=ALU.mult), rd=[PBb[5]], wr=[smb_])
                T.op("dve", lambda e: e.tensor_tensor(out=sm[:, 0:4], in0=sm[:, 24:28], in1=den, op=ALU.max), rd=[PBb[5], smb_], wr=[smb_])
                T.op("dve", lambda e: e.tensor_tensor(out=sm[:, 0:4], in0=sm[:, 0:4], in1=wfl[:, c, 4:8], op=ALU.max), rd=[smb_, wflb], wr=[smb_])
                T.op("dve", lambda e: e.reciprocal(out=sm[:, 4:8], in_=sm[:, 0:4]), rd=[smb_], wr=[smb_])
                for h in range(4):
                    T.op("act", lambda e, h=h: e.activation(out=junk[:], in_=numv[h // 2][:, h % 2, :], func=AF.Square, accum_out=sm[:, 8 + h:9 + h]),
                         rd=[PBb[h // 2]], wr=[junkb, smallb2[b]])
            for h in range(4):
                T.op("dve", lambda e, h=h: e.scalar_tensor_tensor(out=Cst[:, h, :], in0=Cst[:, h, :], scalar=arep[:, 0, 4 * c + h:4 * c + h + 1], in1=dcv[h // 2][:, h % 2, :],
                                                                  op0=ALU.mult, op1=ALU.add), rd=[Cstb, arepb, PBb[2 + h // 2]], wr=[Cstb])
            T.op("dve", lambda e: e.tensor_tensor(out=nst[:], in0=nst[:], in1=arep[:, 0, 4 * c:4 * c + 4], op=ALU.mult), rd=[nstb, arepb], wr=[nstb])
            T.op("dve", lambda e: e.tensor_tensor(out=nst[:], in0=nst[:], in1=dn, op=ALU.add), rd=[nstb, PBb[5]], wr=[nstb])
            T.op("act", lambda e: e.activation(out=Cb_[:].rearrange("p h d -> p (h d)"), in_=Cst[:].rearrange("p h d -> p (h d)"), func=AF.Identity), rd=[Cstb], wr=[Cbb])
            T.op("act", lambda e: e.activation(out=nbf[:], in_=nst[:], func=AF.Identity), rd=[nstb], wr=[nbfb])
            if not state_only:
                pendb = c
        if pendb is not None:
            pend = part_b(pendb)
        if pend is not None:
            emit_ypre_T(*pend)
        ws.release(2 if state_only else 4)

    def xattn_tile(nsub, l):
        Tn = 128 * nsub
        arena_switch('xattn')
        pf_list[0] = [6, 7, 4, 5]
        wqu = [ws.get(("wq", l, 0)), ws.get(("wq", l, 1))]
        for c in range(8):
            w, wb = wqu[c // 4]
            col0 = (c % 4) * 128
            bk = next_pf()
            for kc in range(8):
                T.op("pe", lambda e, kc=kc: e.matmul(PB[bk][:, 0:Tn], lhsT=w[:, kc, col0:col0 + 128], rhs=xT[:, kc, 0:Tn], start=(kc == 0), stop=(kc == 7)),
                     rd=[wb] + xr(XTA, range(nsub)), wr=[PBb[bk]], inc=(kc == 7))
            if c % 2 == 0:
                T.op("act", lambda e: e.activation(out=FA[:, c, 0:Tn], in_=PB[bk][:, 0:Tn], func=AF.Identity, scale=float(256 ** -0.5)), rd=[PBb[bk]], wr=[FAb[c]])
            else:
                T.op("dve", lambda e: e.tensor_scalar(out=FA[:, c, 0:Tn], in0=PB[bk][:, 0:Tn], scalar1=float(256 ** -0.5), scalar2=None, op0=ALU.mult), rd=[PBb[bk]], wr=[FAb[c]])
        ws.release(2)
        pf_list[0] = [6, 7, 4, 5, 2, 3]

        def scores(h):
            ptb = [pt_b[h]]
            for mc in range(2):
                bk = next_pf()
                for hf in range(2):
                    T.op("pe", lambda e, hf=hf: e.matmul(PB[bk][:, 0:Tn], lhsT=KT[:, 2 * h + hf, mc * 128:(mc + 1) * 128], rhs=FA[:, 2 * h + hf, 0:Tn], start=(hf == 0), stop=(hf == 1)),
                         rd=[KTb, FAb[2 * h + hf]], wr=[PBb[bk]], inc=(hf == 1))
                T.op("act", lambda e: e.activation(out=pt_ap[:, h, mc, 0:Tn], in_=PB[bk][:, 0:Tn], func=AF.Exp), rd=[PBb[bk]], wr=ptb)

        scores(0)
        for h in range(4):
            ptb = [pt_b[h]]
            if h < 3:
                scores(h + 1)
            bk = next_pf()
            for mc in range(2):
                T.op("pe", lambda e, mc=mc: e.matmul(PB[bk][:, 0:Tn], lhsT=onesb[:], rhs=pt_ap[:, h, mc, 0:Tn], start=(mc == 0), stop=(mc == 1)), rd=ptb + [cb], wr=[PBb[bk]], inc=(mc == 1))
            rsb = [rs_b[h]]
            T.op("act", lambda e: e.activation(out=rs_ap[:, h, 0:Tn], in_=PB[bk][:, 0:Tn], func=AF.Ln), rd=[PBb[bk]], wr=rsb)
            T.op("act", lambda e: e.activation(out=rs_ap[:, h, 0:Tn], in_=rs_ap[:, h, 0:Tn], func=AF.Exp, scale=-1.0), rd=rsb, wr=rsb)
            for hf in range(2):
                c = 2 * h + hf
                bk = next_pf()
                for mc in range(2):
                    T.op("pe", lambda e, mc=mc: e.matmul(PB[bk][:, 0:Tn], lhsT=Vm[:, mc, c * 128:(c + 1) * 128], rhs=pt_ap[:, h, mc, 0:Tn], start=(mc == 0), stop=(mc == 1)),
                         rd=ptb + [Vmb], wr=[PBb[bk]], inc=(mc == 1))
                T.op("dve", lambda e: e.tensor_tensor(out=FA[:, c, 0:Tn], in0=PB[bk][:, 0:Tn], in1=rs_ap[:, h, 0:Tn], op=ALU.mult), rd=[PBb[bk]] + rsb, wr=[FAb[c]])
        pf_list[0] = [6, 7]

    def mlp_hidden(nsub, l):
        Tn = 128 * nsub
        arena_switch('mlp')
        pf_list[0] = [6, 7, 4, 5]
        for i in range(8):
            w, wb = ws.get(("w1", l, i))
            for ff in range(4):
                f = i * 4 + ff
                bk = next_pf()
                for kc in range(8):
                    T.op("pe", lambda e, kc=kc: e.matmul(PB[bk][:, 0:Tn], lhsT=w[:, kc, ff * 128:(ff + 1) * 128], rhs=xT[:, kc, 0:Tn], start=(kc == 0), stop=(kc == 7)),
                         rd=[wb] + xr(XTA, range(nsub)), wr=[PBb[bk]], inc=(kc == 7))
                k = f % 2
                T.op("act", lambda e: e.activation(out=rtmp[:, k, 0:Tn], in_=PB[bk][:, 0:Tn], func=AF.Relu, bias=b1c[:, l, f:f + 1], scale=1.0), rd=[PBb[bk], cb], wr=[rtmpb[k]])
                T.op("pool", lambda e: e.tensor_tensor(out=hT[:, f, 0:Tn], in0=rtmp[:, k, 0:Tn], in1=rtmp[:, k, 0:Tn], op=ALU.mult), rd=[rtmpb[k]], wr=[hTb[f]])
            ws.release(1)
        pf_list[0] = [6, 7]

    def mlp_proj(l):
        state = {"n": -1, "units": None}

        def proj(s, n, bk, first):
            if state["n"] != n:
                if state["units"] is not None:
                    ws.release(4)
                state["units"] = [ws.get(("w2", l, n, q)) for q in range(4)]
                state["n"] = n
            for f in range(32):
                w, wb = state["units"][f // 8]
                T.op("pe", lambda e, f=f: e.matmul(PB[bk][:], lhsT=hT[:, f, s * 128:(s + 1) * 128], rhs=w[:, f % 8, :], start=(first and f == 0), stop=(f == 31)),
                     rd=[wb, hTb[f]], wr=[PBb[bk]], inc=(f == 31))
        return proj

    def pool_phase(first_tile):
        Tn = 512
        L = 16 + Tn
        for j, wdw in enumerate((2, 4, 8, 16)):
            eng = "dve"
            X = x1T[:, 2 * j:2 * j + 2, :]
            cur, curb = X, [x1b[j]]
            k = 1
            lvl = 0
            while k < wdw:
                nxt = ptmp[:, lvl % 2]
                nb_ = [ptmpb[lvl % 2]]
                lo = 2 * k - 1
                T.op(eng, lambda e, cur=cur, nxt=nxt, lo=lo, k=k: e.tensor_tensor(out=nxt[:, :, lo:L], in0=cur[:, :, lo:L], in1=cur[:, :, lo - k:L - k], op=ALU.add),
                     rd=curb, wr=nb_)
                cur, curb = nxt, nb_
                k *= 2
                lvl += 1
            T.op("dve", lambda e, cur=cur: e.scalar_tensor_tensor(out=FA[:, 2 * j:2 * j + 2, 0:Tn], in0=cur[:, :, 16:L], scalar=1.0 / wdw, in1=X[:, :, 16:L], op0=ALU.mult, op1=ALU.subtract),
                 rd=curb + [x1b[j]], wr=FAb[2 * j:2 * j + 2])
            if first_tile:
                T.op("dve", lambda e, cur=cur: e.tensor_tensor(out=ptiny[:], in0=cur[:, :, 16:32], in1=invct[:, j, :].unsqueeze(1).to_broadcast([128, 2, 16]), op=ALU.mult),
                     rd=curb + [cb], wr=[ptinyb])
                T.op("dve", lambda e: e.tensor_tensor(out=FA[:, 2 * j:2 * j + 2, 0:16], in0=ptiny[:], in1=X[:, :, 16:32], op=ALU.subtract), rd=[ptinyb, x1b[j]], wr=FAb[2 * j:2 * j + 2])

    def pool_proj(s, n, bk, first):
        for jj in range(2):
            j = 2 * n + jj
            for kc in range(2):
                T.op("pe", lambda e, kc=kc: e.matmul(PB[bk][:, jj * 256:(jj + 1) * 256], lhsT=FA[:, 2 * j + kc, s * 128:(s + 1) * 128], rhs=poolw[:, j, kc, :], start=False, stop=(kc == 1)),
                     rd=[FAb[2 * j + kc], poolwb], wr=[PBb[bk]], inc=(jj == 1 and kc == 1))

    def halo_update(Tn, use_flag):
        T.op("pool", lambda e: e.tensor_copy(out=x1T[:, :, 0:16], in_=x1T[:, :, Tn:Tn + 16]), rd=x1b, wr=x1b)
        if use_flag:
            T.op("pool", lambda e: e.tensor_scalar(out=x1T[:, :, 0:16], in0=x1T[:, :, 0:16], scalar1=flg[:, 0:1], scalar2=None, op0=ALU.mult), rd=x1b + [cb], wr=x1b)

    pre_tiles = []
    t0 = 0
    while t0 < n_pre:
        ns = min(4, (n_pre - t0) // 128)
        pre_tiles.append((t0, ns))
        t0 += ns * 128
    main_tiles = [(0, 1, True)] + [(MINI + i * 512, 4, False) for i in range(n_main // 512)]
    ws.plan(["kv0", "kv1", "kv2", "kv3"])
    for _ in pre_tiles:
        ws.plan(["qk1", "v0", "v1"])

    def l_units(l):
        return [("wq", l, 0), ("wq", l, 1), ("wo", l, 0), ("wo", l, 1)] + [("w1", l, i) for i in range(8)] + [("w2", l, n, q) for n in range(2) for q in range(4)]

    for (_, _, mini) in main_tiles:
        ws.plan(["qk0", "qk1", "v0", "v1", "o0", "o1", "wout0", "wout1"] + l_units(0))
        if not mini:
            ws.plan(l_units(1))
    ws.start()

    T.dma(R[:, 0:2, :], memb.rearrange("(s p) d -> p s d", p=128), wr=Rb[0:2])
    for s in range(2):
        feat_major(s, xT, xT_bufs(s))
    kvu = [ws.get("kv%d" % i) for i in range(4)]
    for c in range(8):
        w, wb = kvu[c // 4]
        col0 = (c % 4) * 128
        bk = next_pf()
        for kc in range(8):
            T.op("pe", lambda e, kc=kc: e.matmul(PB[bk][:, 0:256], lhsT=w[:, kc, col0:col0 + 128], rhs=xT[:, kc, 0:256], start=(kc == 0), stop=(kc == 7)),
                 rd=[wb] + xr(XTA, range(2)), wr=[PBb[bk]], inc=(kc == 7))
        T.op("act", lambda e: e.activation(out=KT[:, c, :], in_=PB[bk][:, 0:256], func=AF.Identity), rd=[PBb[bk]], wr=[KTb])
    for mc in range(2):
        for n in range(2):
            w, wb = kvu[2 + n]
            bk = next_pf()
            for kc in range(8):
                T.op("pe", lambda e, kc=kc: e.matmul(PB[bk][:], lhsT=xT[:, kc, mc * 128:(mc + 1) * 128], rhs=w[:, kc, :], start=(kc == 0), stop=(kc == 7)),
                     rd=[wb] + xTb[mc], wr=[PBb[bk]], inc=(kc == 7))
            T.op("dve", lambda e: e.tensor_copy(out=Vm[:, mc, n * 512:(n + 1) * 512], in_=PB[bk][:]), rd=[PBb[bk]], wr=[Vmb])
    ws.release(4)

    i_ps = sg_load(pool_scale[0:1, :])
    T.dma(R[:, 2, :].rearrange("p (j k d) -> p j k d", j=2, k=2), pool_w[0:2].rearrange("j (k p) d -> p j k d", p=128), wr=[Rb[2]])
    T.dma(R[:, 3, :].rearrange("p (j k d) -> p j k d", j=2, k=2), pool_w[2:4].rearrange("j (k p) d -> p j k d", p=128), wr=[Rb[3]])
    for j in range(4):
        src = R[:, 2 + j // 2, :].rearrange("p (j k d) -> p j k d", j=2, k=2)[:, j % 2]
        T.op("dve", lambda e, src=src, j=j: e.tensor_tensor(out=poolw[:, j], in0=src, in1=SG[i_ps][:, j * 256:(j + 1) * 256].unsqueeze(1).to_broadcast([128, 2, 256]), op=ALU.mult),
             rd=[Rb[2 + j // 2], SGb[i_ps]], wr=[poolwb])

    out_st = []
    npre = len(pre_tiles)
    jobs = []
    for i, (tk0, ns) in enumerate(pre_tiles):
        par = (npre - 1 - i) % 2
        jobs.append(dict(src=xp, tk0=tk0, ns=ns, stage=(STR if par == 0 else STX), X=(XTA if par == 0 else XTB), pre=True))
    jobs.append(dict(src=xm, tk0=0, ns=1, stage=STR, X=XTA, pre=False, mini=True))

    def prep(job):
        job["stage"].load(job["src"], job["tk0"], job["ns"])
        for s_ in range(job["ns"]):
            feat_major(s_, job["X"]["ap"], xT_bufs(s_, job["X"]), stage=job["stage"])

    def layer0_rest(ns, res, mini):
        wou = [ws.get("wout0"), ws.get("wout1")]
        ln_stage(ns, tok_proj(FB, lambda s, kc: [FBb[kc]], wou), None, None, 0, dst=xT, dstbufs=xT_bufs, res=res)
        ws.release(2)
        xattn_tile(ns, 0)
        wou = [ws.get(("wo", 0, 0)), ws.get(("wo", 0, 1))]
        ln_stage(ns, tok_proj(FA, lambda s, kc: [FAb[kc]], wou), lng_rows[0:1, :], 0, 1, dst=xT, dstbufs=xT_bufs)
        ws.release(2)
        mlp_hidden(ns, 0)
        ln_stage(ns, mlp_proj(0), lng_rows[1:2, :], 1, 2, dst=x1T, dstbufs=lambda s: (lambda kc: [x1b[kc // 2]]), f32dst=True)
        ws.release(4)

    prep(jobs[0])
    per_tile = (len(cast_jobs) + max(1, npre - 2) - 1) // max(1, npre - 2)
    for i, job in enumerate(jobs):
        ns = job["ns"]
        emit_casts(per_tile)
        mlstm_front(ns, job["pre"], job["X"])
        nxt = jobs[i + 1] if i + 1 < len(jobs) else None
        early = nxt is not None and nxt["X"] is not job["X"] and nxt["stage"] is not job["stage"]
        if early:
            prep(nxt)
        mlstm_loop(ns, job["pre"], job["X"])
        if nxt is not None and not early:
            prep(nxt)
        if job["pre"] and i + 1 < len(jobs) and not jobs[i + 1]["pre"]:
            alias_switch([b for p_ in FXb for b in p_], FBb)
    T.op("dve", lambda e: e.tensor_scalar(out=Cst[:].rearrange("p h d -> p (h d)"), in0=Cst[:].rearrange("p h d -> p (h d)"), scalar1=flg[:, 0:1], scalar2=None, op0=ALU.mult),
         rd=[Cstb, cb], wr=[Cstb])
    T.op("dve", lambda e: e.tensor_scalar(out=nst[:], in0=nst[:], scalar1=flg[:, 0:1], scalar2=None, op0=ALU.mult), rd=[nstb, cb], wr=[nstb])
    T.op("dve", lambda e: e.tensor_scalar(out=gsm[:, 12:13], in0=gsm[:, 12:13], scalar1=flg[0:4, 0:1], scalar2=None, op0=ALU.mult), rd=[gsmb, cb], wr=[gsmb])
    T.op("act", lambda e: e.activation(out=Cb_[:].rearrange("p h d -> p (h d)"), in_=Cst[:].rearrange("p h d -> p (h d)"), func=AF.Identity), rd=[Cstb], wr=[Cbb])
    T.op("act", lambda e: e.activation(out=nbf[:], in_=nst[:], func=AF.Identity), rd=[nstb], wr=[nbfb])
    layer0_rest(1, STR, True)
    halo_update(128, True)

    n_tiles = n_main // 512
    use_stx = cfg.get("stx", True)
    MST = STX if use_stx else STR
    if use_stx:
        STX.load(xm, MINI, 4)
    for ti in range(n_tiles):
        ns, Tn = 4, 512
        if not use_stx:
            STR.load(xm, MINI + ti * 512, 4)
        for s_ in range(ns):
            feat_major(s_, xT, xT_bufs(s_), stage=MST)
        mlstm_front(ns, False, XTA)
        mlstm_loop(ns, False, XTA)
        layer0_rest(ns, MST, False)
        if dbg is not None and ti == dbg.get("_tile", 0):
            dump("l0ln3", R[:, 0:ns, :], Rb[0:ns], [128, ns, D])
        pool_phase(ti == 0)
        ln_stage(ns, pool_proj, lng_rows[2:3, :], 2, 3, dst=xT, dstbufs=xT_bufs)
        halo_update(Tn, False)
        if use_stx and ti + 1 < n_tiles:
            STX.load(xm, MINI + (ti + 1) * 512, 4)
        xattn_tile(ns, 1)
        wou = [ws.get(("wo", 1, 0)), ws.get(("wo", 1, 1))]
        ln_stage(ns, tok_proj(FA, lambda s, kc: [FAb[kc]], wou), lng_rows[3:4, :], 3, 4, dst=xT, dstbufs=xT_bufs)
        ws.release(2)
        mlp_hidden(ns, 1)
        ln_stage(ns, mlp_proj(1), lng_rows[4:5, :], 4, 5, final=True)
        ws.release(4)
        out_st.append(T.dma(out[ti * 512:(ti + 1) * 512, :].rearrange("(s p) d -> p s d", p=128), R[:, 0:ns, :], rd=Rb[0:ns]))

    for key, st in out_st + dumps:
        if T.seen["sp"].get(key, 0) < st:
            T.eng["sp"].wait_ge(T._semof(key), st)
            T.seen["sp"][key] = st
    es.close()
    return nc, T


def host_inputs(x, mem, mlstm_w_in, mlstm_gate_b, mlstm_conv_w, mlstm_norm_g, mlstm_w_out, pool_w, pool_scale, mem_w_kv,
                xattn_wq, xattn_wo, mlp_w1, mlp_b1, mlp_w2, mlp_b2, ln_g, ln_b, n_pre=PRE, n_main=TOK_MAIN, cores=None, seq=SEQ):
    f = lambda a: np.ascontiguousarray(np.asarray(a, dtype=np.float32))
    x, mem = f(x), f(mem)
    ln_g, ln_b = f(ln_g), f(ln_b)
    col = lambda v: np.ascontiguousarray(v.reshape(8, 128).T)
    lng6 = ln_g.reshape(6, D)
    lnb6 = ln_b.reshape(6, D)
    rows8 = np.zeros((128, D), np.float32)
    rows8[0:5] = lnb6[0:5]
    rows8[5] = f(mlp_b2)[0]
    rows8[6] = f(mlp_b2)[1]
    sel = np.zeros((128, 6, 128), np.float32)
    a = np.float32(ALPHA)
    sel[0, 1] = a
    sel[1, 2] = a
    sel[5, 2] = 1.0
    sel[2, 3] = a
    sel[3, 4] = a
    sel[4, 5] = a
    sel[6, 5] = 1.0
    mask = np.zeros((128, 4, 128), np.float32)
    tri = (np.arange(128)[None, :] >= np.arange(128)[:, None]).astype(np.float32)
    mask[:] = tri[:, None, :]
    shared = {
        "w_in": f(mlstm_w_in)[0], "w_out": f(mlstm_w_out)[0], "w_kv": f(mem_w_kv), "wq": f(xattn_wq), "wo": f(xattn_wo),
        "w1": f(mlp_w1), "w2": f(mlp_w2), "pool_w": f(pool_w)[0], "pool_scale": f(pool_scale).reshape(1, D),
        "convw": np.ascontiguousarray(f(mlstm_conv_w)[0].reshape(4, 8, 128).transpose(2, 1, 0)),
        "normg": col(f(mlstm_norm_g)[0]),
        "lng_col": np.ascontiguousarray(np.stack([col(lng6[i]) for i in range(6)], axis=1)),
        "lnb_col": np.ascontiguousarray(np.stack([col(lnb6[i]) for i in range(6)], axis=1)),
        "lng_rows": lng6, "lnb_rows": lnb6,
        "b1_col": np.ascontiguousarray(f(mlp_b1).reshape(2, 32, 128).transpose(2, 0, 1)),
        "gateb": np.ascontiguousarray(f(mlstm_gate_b)[0].reshape(2, 4).T),
        "rows8": rows8, "sel": sel, "ident": np.eye(128, dtype=np.float32), "mask": mask,
    }
    half = seq // 2
    maps = []
    wins = np.array([2, 4, 8, 16], np.float32)
    for core in (range(8) if cores is None else cores):
        b, hf = core // 2, core % 2
        m = dict(shared)
        if hf == 0:
            xmain = np.concatenate([np.zeros((MINI, D), np.float32), x[b, 0:n_main]], axis=0)
            xpre = np.zeros((max(n_pre, 128), D), np.float32)
            cnt = np.minimum(np.arange(16, dtype=np.float32)[None, :] + 1.0, wins[:, None])
            fl = 0.0
        else:
            xmain = x[b, half - MINI: half + n_main]
            xpre = x[b, half - MINI - n_pre: half - MINI] if n_pre > 0 else np.zeros((128, D), np.float32)
            cnt = np.broadcast_to(wins[:, None], (4, 16))
            fl = 1.0
        m["xm"] = np.ascontiguousarray(xmain)
        m["xp"] = np.ascontiguousarray(xpre)
        m["memb"] = mem[b]
        m["flag"] = np.full((128, 1), fl, np.float32)
        m["invc"] = np.ascontiguousarray(np.broadcast_to((1.0 / cnt)[None], (128, 4, 16)).astype(np.float32))
        maps.append(m)
    return maps


_CACHE = {}


def kernel(**inputs):
    if "nc" not in _CACHE:
        _CACHE["nc"] = build({})[0]
    nc = _CACHE["nc"]
    maps = host_inputs(**inputs)
    res = run_bass_kernel_spmd(nc, maps, core_ids=list(range(8)))
    out = np.empty((BATCH, SEQ, D), np.float32)
    for core in range(8):
        b, hf = core // 2, core % 2
        out[b, hf * (SEQ // 2):(hf + 1) * (SEQ // 2)] = res.results[core]["out"]
    return out
```

```python
import numpy as np
import ml_dtypes
from contextlib import ExitStack
import concourse.bass as bass
import concourse.mybir as mybir
from concourse.bass_utils import run_bass_kernel_spmd

F32 = mybir.dt.float32
BF16 = mybir.dt.bfloat16
ALU = mybir.AluOpType
AF = mybir.ActivationFunctionType
AX = mybir.AxisListType

D = 1024
H = 4
DK = 128
DV = 256
DFF = 4096
MEM = 256
SEQ = 8192
BATCH = 4
ALPHA = 4.0 ** 0.25
EPS = 1e-5
import os as _os
EXCL = bool(_os.environ.get('EXCL'))
NSLOT = 6
NDS = 12
TOK_MAIN = 4096
MINI = 128
PRE = SEQ // 2 - MINI


class Buf:
    __slots__ = ("name", "w", "r", "excl")

    def __init__(self, name, excl=False):
        self.name = name
        self.w = None
        self.r = {}
        self.excl = excl


class Trk:
    def __init__(self, nc, es):
        self.nc = nc
        self.eng = {"pe": nc.tensor, "act": nc.scalar, "dve": nc.vector, "pool": nc.gpsimd, "sp": nc.sync}
        self.sem = {k: es.enter_context(nc.semaphore("s_" + k)) for k in self.eng}
        self.cnt = {k: 0 for k in self.eng}
        self.seen = {k: {} for k in self.eng}
        self.dsem = [es.enter_context(nc.semaphore("d%d" % i)) for i in range(NDS)]
        self.dcnt = [0] * NDS
        self.dnext = 0
        self.nwait = 0
        self.csem = []
        self.es = es

    def _semof(self, key):
        if isinstance(key, str):
            return self.sem[key]
        return self.dsem[key[1]] if key[0] == "d" else self.csem[key[1]]

    def dma_once(self, out, in_, q="pool"):
        i = len(self.csem)
        self.csem.append(self.es.enter_context(self.nc.semaphore("c%d" % i)))
        self.eng[q].dma_start(out=out, in_=in_).then_inc(self.csem[i], 16)
        return (("c", i), 16)

    def _needs(self, e, rd, wr):
        needs = {}

        def need(k, v):
            if v > needs.get(k, 0):
                needs[k] = v

        for b in rd:
            if b.w is not None:
                need(*b.w)
            if b.excl and EXCL:
                for k, v in b.r.items():
                    if k != e:
                        need(k, v)
        for b in wr:
            if b.w is not None and b.w[0] != e:
                need(*b.w)
            for k, v in b.r.items():
                if k != e:
                    need(k, v)
        return needs

    def _wait(self, e, needs):
        for key, val in needs.items():
            if key == e and (e == "pe" or val > self.cnt[e]):
                continue
            if self.seen[e].get(key, 0) >= val:
                continue
            self.eng[e].wait_ge(self._semof(key), val)
            self.seen[e][key] = val
            self.nwait += 1

    def op(self, e, fn, rd=(), wr=(), inc=True):
        self._wait(e, self._needs(e, rd, wr))
        ins = fn(self.eng[e])
        if inc:
            self.cnt[e] += 1
            ins.then_inc(self.sem[e], 1)
            st = self.cnt[e]
        else:
            st = self.cnt[e] + 1
        for b in rd:
            if st > b.r.get(e, 0):
                b.r[e] = st
        for b in wr:
            b.w = (e, st)
            b.r = {}
        return ins

    def dma(self, out, in_, rd=(), wr=(), q="sp", extra=()):
        i = self.dnext
        self.dnext = (i + 1) % NDS
        needs = self._needs(q, rd, wr)
        for k, v in extra:
            needs[k] = max(needs.get(k, 0), v)
        key = ("d", i)
        if self.dcnt[i] > 0:
            needs[key] = max(needs.get(key, 0), 16 * self.dcnt[i])
        self._wait(q, needs)
        self.dcnt[i] += 1
        self.eng[q].dma_start(out=out, in_=in_).then_inc(self.dsem[i], 16)
        st = 16 * self.dcnt[i]
        for b in rd:
            b.r[key] = st
        for b in wr:
            b.w = (key, st)
            b.r = {}
        return (key, st)


def build(cfg):
    n_pre = cfg.get("n_pre", PRE)
    n_main = cfg.get("n_main", TOK_MAIN)
    dbg = cfg.get("dbg", None)
    assert n_pre % 128 == 0 and n_main % 512 == 0

    nc = bass.Bass("TRN2", target_bir_lowering=False)
    es = ExitStack()

    def din(name, shape, dt=F32):
        return nc.dram_tensor(name, list(shape), dt, kind="ExternalInput").ap()

    xm = din("xm", [MINI + n_main, D])
    xp = din("xp", [max(n_pre, 128), D])
    memb = din("memb", [MEM, D])
    flag = din("flag", [128, 1])
    invc = din("invc", [128, 4, 16])
    w_in = din("w_in", [D, 3080])
    w_out = din("w_out", [D, D])
    w_kv = din("w_kv", [D, 2 * D])
    wq = din("wq", [2, D, D])
    wo = din("wo", [2, D, D])
    w1 = din("w1", [2, D, DFF])
    w2 = din("w2", [2, DFF, D])
    pool_w = din("pool_w", [4, 256, 256])
    pool_scale = din("pool_scale", [1, D])
    convw_d = din("convw", [128, 8, 4])
    normg_d = din("normg", [128, 8])
    lng_col_d = din("lng_col", [128, 6, 8])
    lnb_col_d = din("lnb_col", [128, 6, 8])
    lng_rows = din("lng_rows", [6, D])
    lnb_rows = din("lnb_rows", [6, D])
    b1_col_d = din("b1_col", [128, 2, 32])
    gateb_d = din("gateb", [4, 2])
    rows8_d = din("rows8", [128, D])
    sel_d = din("sel", [128, 6, 128])
    ident_d = din("ident", [128, 128])
    mask_d = din("mask", [128, 4, 128])
    out = nc.dram_tensor("out", [n_main, D], F32, kind="ExternalOutput").ap()

    uids = ["qk0", "qk1", "v0", "v1", "o0", "o1", "wout0", "wout1", "kv0", "kv1", "kv2", "kv3"]
    for l in range(2):
        uids += [("wq", l, 0), ("wq", l, 1), ("wo", l, 0), ("wo", l, 1)]
        uids += [("w1", l, i) for i in range(8)]
        uids += [("w2", l, n, q) for n in range(2) for q in range(4)]
    uidx = {u: i for i, u in enumerate(uids)}
    wsc = nc.dram_tensor("wsc", [len(uids), 128, 8, 512], BF16, kind="Internal").ap()

    T = Trk(nc, es)

    def sb(name, shape, dt=F32):
        return es.enter_context(nc.sbuf_tensor("sb_" + name, list(shape), dt))

    ring = [sb("ring%d" % i, [128, 8, 512], BF16) for i in range(NSLOT)]
    ringb = [Buf("ring%d" % i) for i in range(NSLOT)]
    R = sb("R", [128, 4, D])
    Rb = [Buf("R%d" % i) for i in range(4)]
    xT = sb("xT", [128, 8, 512], BF16)
    xTb = [[Buf("xTe%d" % i), Buf("xTo%d" % i)] for i in range(4)]
    FA = sb("FA", [128, 8, 512], BF16)
    FAb = [Buf("FA%d" % i) for i in range(8)]
    FB = sb("FB", [128, 8, 512], BF16)
    FBb = [Buf("FB%d" % i) for i in range(8)]
    SG = [sb("SG%d" % i, [128, D]) for i in range(3)]
    SGb = [Buf("SG%d" % i) for i in range(3)]
    sgn = [0]
    hT = sb("hT", [128, 32, 512], BF16)
    hTb = [Buf("hT%d" % i) for i in range(32)]
    hflat = hT[:].rearrange("p a b -> p (a b)")

    def carve(kb0, nbytes, shape, dt):
        n_el = nbytes // 2
        ap = hflat[:, kb0 * 512: kb0 * 512 + n_el]
        if dt == F32:
            ap = ap.bitcast(F32)
        if len(shape) == 1:
            ap = ap.rearrange("p (a b) -> p a b", b=shape[0])
        elif len(shape) == 2:
            ap = ap.rearrange("p (a b c) -> p a b c", b=shape[0], c=shape[1])
        return ap

    def bufs2(name, n=2):
        return [Buf("%s%d" % (name, i)) for i in range(n)]

    vw_ap, vw_b = carve(0, 4096, (4, 256), BF16), bufs2("vw")
    sgo_ap, sgo_b = carve(4, 4096, (1024,), BF16), bufs2("sgo")
    raw_ap, raw_b = carve(8, 2 * 516 * 4, (516,), F32), bufs2("raw")
    acc_ap, acc_b = carve(13, 4096, (512,), F32), bufs2("acc")
    ypre_ap, ypre_b = carve(17, 4096, (1024,), BF16), bufs2("ypre")
    ktok_ap, ktok_b = carve(21, 2048, (4, 128), BF16), bufs2("ktok")
    smt_ap, smt_b = carve(23, 2048, (4, 128), BF16), bufs2("smt")
    qat_ap, qat_b = carve(25, 2048, (4, 128), BF16), bufs2("qat")
    rs_ap, rs_b = carve(0, 8192, (512,), F32), bufs2("rs", 4)
    pt_ap, pt_b = carve(8, 8192, (2, 512), BF16), bufs2("pt", 4)
    arena = vw_b + sgo_b + raw_b + acc_b + ypre_b + ktok_b + smt_b + qat_b + rs_b + pt_b + hTb
    arena_owner = [None]

    def arena_switch(owner):
        if arena_owner[0] == owner:
            return
        arena_owner[0] = owner
        merged = {}
        for b in arena:
            if b.w is not None and b.w[1] > merged.get(b.w[0], 0):
                merged[b.w[0]] = b.w[1]
            for k, v in b.r.items():
                if v > merged.get(k, 0):
                    merged[k] = v
        for b in arena:
            b.w = None
            b.r = dict(merged)

    rtmp = sb("rtmp", [128, 2, 512])
    rtmpb = [Buf("rtmp0"), Buf("rtmp1")]

    def half(bufs, i):
        return [bufs[i]]

    def gt(name, n):
        return sb(name, [4, n]), Buf(name)

    gI, gIb = gt("gI", 512)
    gE, gEb = gt("gE", 512)
    gCS, gCSb = gt("gCS", 512)
    gU, gUb = gt("gU", 512)
    gW, gWb = gt("gW", 512)
    gFL, gFLb = gt("gFL", 512)
    gsm = sb("gsm", [4, 64])
    gsmb = Buf("gsm")
    ones4 = sb("ones4", [4, 128])
    ones4b = Buf("ones4")
    wfl = sb("wfl", [128, 4, 8])
    wflb = Buf("wfl")
    wcol = sb("wcol", [128, 4, 4], BF16)
    wcolb = Buf("wcol")
    arep = sb("arep", [128, 2, 16])
    arepb = Buf("arep")
    Cst = sb("Cst", [128, 4, 256])
    Cb_ = sb("Cbf", [128, 4, 256], BF16)
    nst = sb("nst", [128, 4])
    nbf = sb("nbf", [128, 4], BF16)
    Cstb, Cbb, nstb, nbfb = Buf("Cst"), Buf("Cbf"), Buf("nst"), Buf("nbf")
    halo = sb("halo", [128, 8, 3])
    halob = Buf("halo")
    small = sb("small", [128, 2, 32])
    smallb = [Buf("small0"), Buf("small1")]
    smallb2 = [Buf("small0b"), Buf("small1b")]
    lnst = sb("lnst", [128, 4, 16])
    lnstb = [Buf("lnst%d" % i) for i in range(4)]
    junk = sb("junk", [128, 256])
    junkb = Buf("junk")
    x1T = sb("x1T", [128, 8, 528])
    x1b = [Buf("x1T%d" % i) for i in range(4)]
    ptmp = sb("ptmp", [128, 2, 2, 528])
    ptmpb = [Buf("ptmp0"), Buf("ptmp1")]
    ptiny = sb("ptiny", [128, 2, 16])
    ptinyb = Buf("ptiny")
    poolw = sb("poolw", [128, 4, 2, 256], BF16)
    poolwb = Buf("poolw")
    KT = sb("KT", [128, 8, 256], BF16)
    Vm = sb("Vm", [128, 2, 1024], BF16)
    KTb, Vmb = Buf("KT"), Buf("Vm")
    ident = sb("ident", [128, 128])
    identb_ = sb("identbf", [128, 128], BF16)
    maskt = sb("maskt", [128, 4, 128], BF16)
    onesb = sb("onesb", [128, 128], BF16)
    convw = sb("convw", [128, 8, 4])
    normg = sb("normgc", [128, 8])
    lngc = sb("lngc", [128, 6, 8])
    lnbc = sb("lnbc", [128, 6, 8])
    b1c = sb("b1c", [128, 2, 32])
    gateb = sb("gatebt", [4, 4])
    rows8 = sb("rows8", [128, D], BF16)
    sel = sb("selt", [128, 6, 128], BF16)
    wg = sb("wg", [128, 8, 8], BF16)
    flg = sb("flg", [128, 1])
    invct = sb("invct", [128, 4, 16])
    epsc = sb("epsc", [128, 8])
    cb = Buf("consts")

    PB = [es.enter_context(nc.psum_tensor("pb%d" % i, [128, 512], F32)) for i in range(8)]
    PBb = [Buf("pb%d" % i, excl=True) for i in range(8)]
    pb5lock = Buf("pb5lock")
    pfn = [0]
    pf_list = [[6, 7]]

    def next_pf():
        l = pf_list[0]
        i = l[pfn[0] % len(l)]
        pfn[0] += 1
        return i

    def pbf(i):
        return PB[i][:].bitcast(BF16)

    dumps = []

    def dump(name, ap, bufs, shape, dt=F32):
        if dbg is None or name not in dbg:
            return
        t = nc.dram_tensor("dbg_" + name, list(shape), dt, kind="ExternalOutput").ap()
        dumps.append(T.dma(t, ap, rd=bufs))

    stage = R[:, 0, :]

    def ld(dst, src, b=cb):
        return T.dma(dst, src, wr=[b])

    ld(ident[:], ident_d[:, :])
    ld(convw[:], convw_d[:, :, :])
    ld(normg[:], normg_d[:, :])
    ld(lngc[:], lng_col_d[:, :, :])
    ld(lnbc[:], lnb_col_d[:, :, :])
    ld(b1c[:], b1_col_d[:, :, :])
    ld(gateb[:, 0:2], gateb_d[:, :])
    ld(flg[:], flag[:, :])
    ld(invct[:], invc[:, :, :])
    T.dma(R[:, 0, 0:512], mask_d.rearrange("p a b -> p (a b)"), wr=[Rb[0]])
    T.dma(R[:, 1, :], rows8_d[:, :], wr=[Rb[1]])
    T.dma(R[:, 2, 0:768], sel_d.rearrange("p a b -> p (a b)"), wr=[Rb[2]])
    T.dma(R[:, 3, 0:64].rearrange("p (a b) -> p a b", b=8), w_in[:, 3072:3080].rearrange("(kc p) g -> p kc g", p=128), wr=[Rb[3]])
    T.op("dve", lambda e: e.tensor_copy(out=maskt[:].rearrange("p a b -> p (a b)"), in_=R[:, 0, 0:512]), rd=[Rb[0]], wr=[cb])
    T.op("dve", lambda e: e.tensor_copy(out=rows8[:], in_=R[:, 1, :]), rd=[Rb[1]], wr=[cb])
    T.op("dve", lambda e: e.tensor_copy(out=sel[:].rearrange("p a b -> p (a b)"), in_=R[:, 2, 0:768]), rd=[Rb[2]], wr=[cb])
    T.op("dve", lambda e: e.tensor_copy(out=wg[:].rearrange("p a b -> p (a b)"), in_=R[:, 3, 0:64]), rd=[Rb[3]], wr=[cb])
    T.op("dve", lambda e: e.tensor_copy(out=identb_[:], in_=ident[:]), rd=[cb], wr=[cb])
    T.op("pool", lambda e: e.memset(onesb[:], 1.0), wr=[cb])
    T.op("pool", lambda e: e.memset(ones4[:], 1.0), wr=[ones4b])
    T.op("pool", lambda e: e.memset(epsc[:, 0:1], EPS), wr=[cb])
    T.op("pool", lambda e: e.memset(epsc[:, 1:2], 1.0), wr=[cb])
    T.op("pool", lambda e: e.memset(epsc[:, 2:3], float(np.log(DK ** -0.5))), wr=[cb])
    T.op("pool", lambda e: e.memset(epsc[:, 4:8], -0.5), wr=[cb])
    T.op("dve", lambda e: e.tensor_scalar(out=gateb[:, 2:3], in0=gateb[:, 1:2], scalar1=-1.0, scalar2=None, op0=ALU.mult), rd=[cb], wr=[cb])
    T.op("pool", lambda e: e.memset(Cst[:], 0.0), wr=[Cstb])
    T.op("pool", lambda e: e.memset(Cb_[:], 0.0), wr=[Cbb])
    T.op("pool", lambda e: e.memset(nst[:], 0.0), wr=[nstb])
    T.op("pool", lambda e: e.memset(nbf[:], 0.0), wr=[nbfb])
    T.op("pool", lambda e: e.memset(halo[:], 0.0), wr=[halob])
    T.op("pool", lambda e: e.memset(gsm[:], 0.0), wr=[gsmb])
    T.op("pool", lambda e: e.memset(x1T[:, :, 0:16], 0.0), wr=x1b)

    cast_st = {}
    cast_jobs = []

    def cast_units(src2d, u0, nu):
        for n in range(nu):
            cast_jobs.append((u0 + n, src2d[:, n * 512:(n + 1) * 512].rearrange("(kc p) j -> p kc j", p=128)))

    cast_units(w_kv, uidx["kv0"], 4)
    cast_units(w_in[:, 512:1024], uidx["qk1"], 1)
    cast_units(w_in[:, 1024:2048], uidx["v0"], 2)
    cast_units(w_in[:, 0:512], uidx["qk0"], 1)
    cast_units(w_in[:, 2048:3072], uidx["o0"], 2)
    cast_units(w_out, uidx["wout0"], 2)
    for l in range(2):
        cast_units(wq[l], uidx[("wq", l, 0)], 2)
        cast_units(wo[l], uidx[("wo", l, 0)], 2)
        cast_units(w1[l], uidx[("w1", l, 0)], 8)
        for n in range(2):
            for q in range(4):
                cast_jobs.append((uidx[("w2", l, n, q)], w2[l][q * 1024:(q + 1) * 1024, n * 512:(n + 1) * 512].rearrange("(kc p) j -> p kc j", p=128)))

    def emit_casts(k):
        for _ in range(k):
            if cast_jobs:
                u, src = cast_jobs.pop(0)
                cast_st[u] = T.dma_once(wsc[u], src, q="pool")

    emit_casts(7)

    class WS:
        def __init__(self):
            self.seq = []
            self.pos_load = 0
            self.pos_use = 0
            self.released = 0

        def plan(self, lst):
            self.seq += lst

        def _load(self):
            j = self.pos_load
            if j >= len(self.seq):
                return
            slot = j % NSLOT
            u = uidx[self.seq[j]]
            while u not in cast_st:
                emit_casts(1)
            T.dma(ring[slot][:], wsc[u], wr=[ringb[slot]], extra=[cast_st[u]])
            self.pos_load += 1

        def start(self):
            while self.pos_load < min(NSLOT, len(self.seq)):
                self._load()

        def get(self, uid):
            assert self.seq[self.pos_use] == uid, (self.seq[self.pos_use], uid)
            assert self.pos_use < self.pos_load, "unit not loaded (ring too small for this use pattern)"
            slot = self.pos_use % NSLOT
            self.pos_use += 1
            return ring[slot], ringb[slot]

        def release(self, k=1):
            for _ in range(k):
                self.released += 1
                assert self.released <= self.pos_use
                if self.pos_load < self.released + NSLOT:
                    self._load()

    ws = WS()

    class StageR:
        def sub(self, s, lo, hi):
            return R[:, s, lo:hi]

        def bufs(self, s):
            return [Rb[s]]

        def load(self, src, tok0, nsub):
            T.dma(R[:, 0:nsub, :], src[tok0:tok0 + 128 * nsub, :].rearrange("(s p) d -> p s d", p=128), wr=Rb[0:nsub])

    class StageX:
        def sub(self, s, lo, hi):
            n = lo // 512
            assert (hi - 1) // 512 == n
            return x1T[:, 2 * s + n, 16 + lo - n * 512: 16 + hi - n * 512]

        def bufs(self, s):
            return [x1b[s]]

        def load(self, src, tok0, nsub):
            for s_ in range(nsub):
                T.dma(x1T[:, 2 * s_:2 * s_ + 2, 16:528], src[tok0 + 128 * s_:tok0 + 128 * (s_ + 1), :].rearrange("p (n j) -> p n j", j=512), wr=[x1b[s_]])

    STR, STX = StageR(), StageX()

    def feat_major(s, dst, dstbufs, q=None, f32dst=False, stage=None):
        stage = stage or STR
        pv = PB[4][:].rearrange("p (a b) -> p a b", b=128)
        pv2 = PB[5][:].rearrange("p (a b) -> p a b", b=128)
        for kc in range(8):
            o = (pv if kc < 4 else pv2)[:, kc % 4, :]
            T.op("pe", lambda e, o=o, kc=kc: e.transpose(out=o, in_=stage.sub(s, kc * 128, (kc + 1) * 128), identity=ident[:]),
                 rd=stage.bufs(s) + [cb], wr=[PBb[4 if kc < 4 else 5]], inc=(kc % 4 == 3))
        off = 16 if f32dst else 0
        for kc in range(8):
            src = (pv if kc < 4 else pv2)[:, kc % 4, :]
            d = dst[:, kc, off + s * 128: off + (s + 1) * 128]
            pb = PBb[4 if kc < 4 else 5]
            if q is None:
                if kc < 4:
                    T.op("dve", lambda e, d=d, src=src: e.tensor_copy(out=d, in_=src), rd=[pb], wr=dstbufs(kc))
                else:
                    T.op("act", lambda e, d=d, src=src: e.activation(out=d, in_=src, func=AF.Identity), rd=[pb], wr=dstbufs(kc))
            else:
                g = lngc[:, q, kc:kc + 1]
                b = lnbc[:, q, kc:kc + 1]
                if kc < 4:
                    T.op("dve", lambda e, d=d, src=src, g=g, b=b: e.tensor_scalar(out=d, in0=src, scalar1=g, scalar2=b, op0=ALU.mult, op1=ALU.add),
                         rd=[pb, cb], wr=dstbufs(kc))
                else:
                    T.op("act", lambda e, d=d, src=src, g=g, b=b: e.activation(out=d, in_=src, func=AF.Identity, bias=b, scale=g),
                         rd=[pb, cb], wr=dstbufs(kc))

    FXb = [[Buf("FXe%d" % i), Buf("FXo%d" % i)] for i in range(4)]
    XTA = {"ap": xT, "b": xTb}
    XTB = {"ap": FB, "b": FXb}

    def xr(X, subs):
        out_ = []
        for s_ in subs:
            out_ += X["b"][s_]
        return out_

    def xT_bufs(s, X=None):
        X = X or XTA
        return lambda kc: [X["b"][s][kc // 4]]

    def alias_switch(old, new):
        merged = {}
        for b in old:
            if b.w is not None and b.w[1] > merged.get(b.w[0], 0):
                merged[b.w[0]] = b.w[1]
            for k, v in b.r.items():
                if v > merged.get(k, 0):
                    merged[k] = v
        for b in new:
            for k, v in merged.items():
                if v > b.r.get(k, 0):
                    b.r[k] = v

    def sg_load(row):
        i = sgn[0] % 3
        sgn[0] += 1
        T.dma(SG[i][:], row.partition_broadcast(128), wr=[SGb[i]])
        return i

    def ln_stage(nsub, proj, sg_row, selq, q, dst=None, dstbufs=None, f32dst=False, final=False, res=None):
        sgi = sg_load(sg_row) if sg_row is not None else None
        if final:
            gi = sg_load(lng_rows[5:6, :])
            bi = sg_load(lnb_rows[5:6, :])
        def post(s):
            if final:
                T.op("pool", lambda e: e.tensor_tensor(out=R[:, s, :], in0=R[:, s, :], in1=SG[gi][:], op=ALU.mult), rd=[Rb[s], SGb[gi]], wr=[Rb[s]])
                T.op("dve", lambda e: e.tensor_tensor(out=R[:, s, :], in0=R[:, s, :], in1=SG[bi][:], op=ALU.add), rd=[Rb[s], SGb[bi]], wr=[Rb[s]])
            else:
                feat_major(s, dst, dstbufs(s), q=q, f32dst=f32dst)

        for n in range(2):
            cs = slice(n * 512, (n + 1) * 512)
            for s in range(nsub):
                bk = s
                first = True
                if selq is not None:
                    T.op("pe", lambda e, bk=bk: e.matmul(PB[bk][:], lhsT=sel[:, q, :], rhs=rows8[:, cs], start=True, stop=False),
                         rd=[cb], wr=[PBb[bk]], inc=False)
                    first = False
                proj(s, n, bk, first)
                if sgi is not None:
                    T.op("pool", lambda e: e.tensor_tensor(out=R[:, s, cs], in0=R[:, s, cs], in1=SG[sgi][:, cs], op=ALU.mult),
                         rd=[Rb[s], SGb[sgi]], wr=[Rb[s]])
                rsrc = R[:, s, cs] if res is None else res.sub(s, n * 512, (n + 1) * 512)
                rbufs = [Rb[s]] if res is None else res.bufs(s)
                T.op("dve", lambda e, bk=bk: e.scalar_tensor_tensor(out=R[:, s, cs], in0=rsrc, scalar=ALPHA, in1=PB[bk][:], op0=ALU.mult, op1=ALU.add),
                     rd=rbufs + [PBb[bk]], wr=[Rb[s]])
                st = lnst[:, s, :]
                sbk = lnstb[s]
                if n == 0:
                    T.op("dve", lambda e: e.bn_stats(out=st[:, 0:6], in_=R[:, s, 0:512]), rd=[Rb[s]], wr=[sbk])
                if n == 1:
                    T.op("dve", lambda e: e.bn_stats(out=st[:, 6:12], in_=R[:, s, 512:1024]), rd=[Rb[s]], wr=[sbk])
                    T.op("dve", lambda e: e.bn_aggr(out=st[:, 12:14], in_=st[:, 0:12]), rd=[sbk], wr=[sbk])
                    T.op("act", lambda e: e.activation(out=st[:, 14:15], in_=st[:, 13:14], func=AF.Sqrt, bias=epsc[:, 0:1], scale=1.0), rd=[sbk, cb], wr=[sbk])
                    T.op("dve", lambda e: e.reciprocal(out=st[:, 14:15], in_=st[:, 14:15]), rd=[sbk], wr=[sbk])
                    T.op("dve", lambda e: e.tensor_scalar(out=st[:, 15:16], in0=st[:, 12:13], scalar1=-1.0, scalar2=st[:, 14:15], op0=ALU.mult, op1=ALU.mult),
                         rd=[sbk], wr=[sbk])
                    T.op("act", lambda e: e.activation(out=R[:, s, :], in_=R[:, s, :], func=AF.Identity, bias=st[:, 15:16], scale=st[:, 14:15]),
                         rd=[Rb[s], sbk], wr=[Rb[s]])
                    if s > 0:
                        post(s - 1)
        post(nsub - 1)

    def tok_proj(srcT, srcbufs, units, kcn=8):
        def proj(s, n, bk, first):
            w, wb = units[n]
            for kc in range(kcn):
                T.op("pe", lambda e, kc=kc: e.matmul(PB[bk][:], lhsT=srcT[:, kc, s * 128:(s + 1) * 128], rhs=w[:, kc, :],
                                                      start=(first and kc == 0), stop=(kc == kcn - 1)),
                     rd=[wb] + srcbufs(s, kc), wr=[PBb[bk]], inc=(kc == kcn - 1))
        return proj

    def gates(nsub, X):
        Tn = 128 * nsub
        xT = X['ap']
        for gi_, (bk, c0) in enumerate(((6, 0), (7, 4))):
            for kc in range(8):
                T.op("pe", lambda e, kc=kc: e.matmul(PB[bk][0:4, 0:Tn], lhsT=wg[:, kc, c0:c0 + 4], rhs=xT[:, kc, 0:Tn], start=(kc == 0), stop=(kc == 7)),
                     rd=[cb] + xr(X, range(nsub)), wr=[PBb[bk]], inc=(kc == 7))
        yield
        T.op("act", lambda e: e.activation(out=gI[:, 0:Tn], in_=PB[6][0:4, 0:Tn], func=AF.Identity, bias=gateb[:, 0:1], scale=1.0), rd=[PBb[6], cb], wr=[gIb])
        T.op("act", lambda e: e.activation(out=gE[:, 0:Tn], in_=PB[7][0:4, 0:Tn], func=AF.Exp, bias=gateb[:, 2:3], scale=-1.0), rd=[PBb[7], cb], wr=[gEb])
        T.op("act", lambda e: e.activation(out=gE[:, 0:Tn], in_=gE[:, 0:Tn], func=AF.Ln, bias=epsc[0:4, 1:2], scale=1.0), rd=[gEb, cb], wr=[gEb])
        yield
        for c in range(nsub):
            T.op("dve", lambda e, c=c: e.tensor_tensor_scan(out=gCS[:, c * 128:(c + 1) * 128], data0=ones4[:, :], data1=gE[:, c * 128:(c + 1) * 128],
                                                            initial=0.0, op0=ALU.mult, op1=ALU.add), rd=[gEb, ones4b], wr=[gCSb])
        T.op("dve", lambda e: e.tensor_tensor(out=gU[:, 0:Tn], in0=gI[:, 0:Tn], in1=gCS[:, 0:Tn], op=ALU.add), rd=[gIb, gCSb], wr=[gUb])
        T.op("dve", lambda e: e.tensor_reduce(out=gsm[:, 0:nsub], in_=gU[:, 0:Tn].rearrange("p (c t) -> p c t", t=128), axis=AX.X, op=ALU.max), rd=[gUb], wr=[gsmb])
        yield
        for c in range(nsub):
            T.op("dve", lambda e, c=c: e.tensor_tensor(out=gsm[:, 4 + c:5 + c], in0=gsm[:, 12:13], in1=gsm[:, c:c + 1], op=ALU.max), rd=[gsmb], wr=[gsmb])
            T.op("dve", lambda e, c=c: e.tensor_tensor(out=gsm[:, 8 + c:9 + c], in0=gsm[:, 12:13], in1=gsm[:, 4 + c:5 + c], op=ALU.subtract), rd=[gsmb], wr=[gsmb])
            T.op("dve", lambda e, c=c: e.tensor_tensor(out=gsm[:, 12:13], in0=gsm[:, 4 + c:5 + c], in1=gCS[:, c * 128 + 127:c * 128 + 128], op=ALU.subtract),
                 rd=[gsmb, gCSb], wr=[gsmb])
        yield
        mcb = gsm[:, 4:4 + nsub].unsqueeze(2).to_broadcast([4, nsub, 128])
        T.op("dve", lambda e: e.tensor_tensor(out=gW[:, 0:Tn].rearrange("p (c t) -> p c t", t=128), in0=gU[:, 0:Tn].rearrange("p (c t) -> p c t", t=128),
                                              in1=mcb, op=ALU.subtract), rd=[gUb, gsmb], wr=[gWb])
        T.op("dve", lambda e: e.tensor_tensor(out=gFL[:, 0:Tn].rearrange("p (c t) -> p c t", t=128), in0=gCS[:, 0:Tn].rearrange("p (c t) -> p c t", t=128),
                                              in1=mcb, op=ALU.subtract), rd=[gCSb, gsmb], wr=[gFLb])
        T.op("act", lambda e: e.activation(out=gW[:, 0:Tn], in_=gW[:, 0:Tn], func=AF.Exp), rd=[gWb], wr=[gWb])
        T.op("act", lambda e: e.activation(out=gFL[:, 0:Tn], in_=gFL[:, 0:Tn], func=AF.Exp), rd=[gFLb], wr=[gFLb])
        yield
        T.op("dve", lambda e: e.tensor_tensor(out=gsm[:, 16:16 + 4 * nsub].rearrange("p (c h) -> p c h", h=4),
                                              in0=gsm[:, 8:8 + nsub].unsqueeze(2).to_broadcast([4, nsub, 4]),
                                              in1=ident[0:4, 0:4].unsqueeze(1).to_broadcast([4, nsub, 4]), op=ALU.mult), rd=[gsmb, cb], wr=[gsmb])
        pv = PB[5][:, 0:8 * nsub].rearrange("p (c j) -> p c j", j=8)
        for c in range(nsub):
            T.op("pe", lambda e, c=c: e.transpose(out=pv[:, c, 0:4], in_=gW[:, c * 128:(c + 1) * 128], identity=ident[0:4, 0:4]), rd=[gWb, cb], wr=[PBb[5]], inc=False)
            T.op("pe", lambda e, c=c: e.transpose(out=pv[:, c, 4:8], in_=gFL[:, c * 128:(c + 1) * 128], identity=ident[0:4, 0:4]), rd=[gFLb, cb], wr=[PBb[5]], inc=False)
        T.op("pe", lambda e: e.matmul(PB[5][:, 64:64 + 4 * nsub], lhsT=ones4[:, :], rhs=gsm[:, 16:16 + 4 * nsub], start=True, stop=True), rd=[ones4b, gsmb], wr=[PBb[5]])
        yield
        T.op("dve", lambda e: e.tensor_copy(out=wfl[:, 0:nsub, :], in_=pv), rd=[PBb[5]], wr=[wflb, pb5lock])
        T.op("act", lambda e: e.activation(out=wcol[:, 0:nsub, :], in_=pv[:, :, 0:4], func=AF.Identity), rd=[PBb[5]], wr=[wcolb, pb5lock])
        T.op("act", lambda e: e.activation(out=arep[:, 0, 0:4 * nsub], in_=PB[5][:, 64:64 + 4 * nsub], func=AF.Exp), rd=[PBb[5]], wr=[arepb, pb5lock])
        T.op("act", lambda e: e.activation(out=arep[:, 1, 0:4 * nsub], in_=PB[5][:, 64:64 + 4 * nsub], func=AF.Exp, bias=epsc[:, 2:3], scale=1.0), rd=[PBb[5], cb], wr=[arepb, pb5lock])

    def qk_conv(nsub, chunks, wunits, X, hook=None):
        Tn = 128 * nsub
        xT = X['ap']
        pend = None
        for c in chunks:
            if hook is not None:
                hook()
            w, wb = wunits[c // 4]
            col0 = (c % 4) * 128
            bk = next_pf()
            b = c % 2
            rawv = raw_ap[:, b, :]
            accv = acc_ap[:, b, 0:Tn]
            rb = half(raw_b, b)
            ab = half(acc_b, b)
            for kc in range(8):
                T.op("pe", lambda e, kc=kc: e.matmul(PB[bk][:, 0:Tn], lhsT=w[:, kc, col0:col0 + 128], rhs=xT[:, kc, 0:Tn], start=(kc == 0), stop=(kc == 7)),
                     rd=[wb] + xr(X, range(nsub)), wr=[PBb[bk]], inc=(kc == 7))
            T.op("pool", lambda e: e.tensor_copy(out=rawv[:, 0:3], in_=halo[:, c, :]), rd=[halob], wr=rb)
            T.op("act", lambda e: e.activation(out=rawv[:, 3:3 + Tn], in_=PB[bk][:, 0:Tn], func=AF.Identity), rd=[PBb[bk]], wr=rb)
            T.op("act", lambda e: e.activation(out=accv, in_=PB[bk][:, 0:Tn], func=AF.Identity, scale=convw[:, c, 3:4]), rd=[PBb[bk], cb], wr=ab)
            T.op("pool", lambda e: e.tensor_copy(out=halo[:, c, :], in_=rawv[:, Tn:Tn + 3]), rd=rb, wr=[halob])
            for j in (2, 1, 0):
                T.op("dve", lambda e, j=j: e.scalar_tensor_tensor(out=accv, in0=rawv[:, j:j + Tn], scalar=convw[:, c, j:j + 1], in1=accv, op0=ALU.mult, op1=ALU.add),
                     rd=rb + ab + [cb], wr=ab)
            if pend is not None:
                pc, pacc, pab = pend
                T.op("act", lambda e: e.activation(out=FA[:, pc, 0:Tn], in_=pacc, func=AF.Silu), rd=pab, wr=[FAb[pc]])
            pend = (c, accv, ab)
        pc, pacc, pab = pend
        T.op("act", lambda e: e.activation(out=FA[:, pc, 0:Tn], in_=pacc, func=AF.Silu), rd=pab, wr=[FAb[pc]])

    def mlstm_front(nsub, state_only, X):
        arena_switch('mlstm')
        pf_list[0] = [6, 7]
        gg = gates(nsub, X)
        for _ in gg:
            pass
        hook = None
        if state_only:
            qk1 = ws.get("qk1")
            qk_conv(nsub, range(4, 8), {1: qk1}, X, hook)
            ws.release(1)
        else:
            qk0 = ws.get("qk0")
            qk1 = ws.get("qk1")
            qk_conv(nsub, range(0, 8), {0: qk0, 1: qk1}, X, hook)
            ws.release(2)
        for _ in gg:
            pass

    def mlstm_loop(nsub, state_only, X):
        Tn = 128 * nsub
        xT = X['ap']
        vu = [ws.get("v0"), ws.get("v1")]
        ou = None if state_only else [ws.get("o0"), ws.get("o1")]
        numv = [PB[0][:].rearrange("p (h d) -> p h d", d=256), PB[1][:].rearrange("p (h d) -> p h d", d=256)]
        dcv = [PB[2][:].rearrange("p (h d) -> p h d", d=256), PB[3][:].rearrange("p (h d) -> p h d", d=256)]
        den = PB[5][:, 128:132]
        dn = PB[5][:, 136:140]
        s4 = PB[4][:].rearrange("p (h t) -> p h t", t=128)

        def emit_ypre_T(pc, pb_):
            ypv = ypre_ap[:, pb_, :]
            ypb = half(ypre_b, pb_)
            bk = next_pf()
            yv = pbf(bk)[:, 0:1024].rearrange("p (a t) -> p a t", t=128)
            for kc in range(8):
                T.op("pe", lambda e, kc=kc: e.transpose(out=yv[:, kc, :], in_=ypv[:, kc * 128:(kc + 1) * 128], identity=identb_[:]), rd=ypb + [cb], wr=[PBb[bk]], inc=(kc == 7))
            T.op("dve", lambda e: e.tensor_tensor(out=FB[:, :, pc * 128:(pc + 1) * 128], in0=yv, in1=normg[:, :].unsqueeze(2).to_broadcast([128, 8, 128]), op=ALU.mult),
                 rd=[PBb[bk], cb], wr=FBb)

        def front_part(c):
            b = c % 2
            ccols = slice(c * 128, (c + 1) * 128)
            if not state_only:
                smv = smt_ap[:, b]
                smb = half(smt_b, b)
                qav = qat_ap[:, b]
                qab = half(qat_b, b)
                for h in range(4):
                    T.op("pe", lambda e, h=h: e.matmul(s4[:, h, :], lhsT=FA[:, 4 + h, ccols], rhs=FA[:, h, ccols], start=True, stop=True),
                         rd=[FAb[4 + h], FAb[h]], wr=[PBb[4]], inc=(h == 3))
                T.op("dve", lambda e: e.scalar_tensor_tensor(out=smv, in0=s4, scalar=float(DK ** -0.5), in1=maskt[:], op0=ALU.mult, op1=ALU.mult),
                     rd=[PBb[4], cb], wr=smb)
                T.op("pool", lambda e: e.tensor_tensor(out=qav, in0=FA[:, 0:4, ccols], in1=arep[:, 1, 4 * c:4 * c + 4].unsqueeze(2).to_broadcast([128, 4, 128]), op=ALU.mult),
                     rd=FAb[0:4] + [arepb], wr=qab)
            bk = next_pf()
            kv_ = pbf(bk)[:, 0:512].rearrange("p (h d) -> p h d", d=128)
            for h in range(4):
                T.op("pe", lambda e, h=h: e.transpose(out=kv_[:, h, :], in_=FA[:, 4 + h, ccols], identity=identb_[:]), rd=[FAb[4 + h], cb], wr=[PBb[bk]], inc=(h == 3))
            T.op("act", lambda e: e.activation(out=ktok_ap[:, b], in_=kv_, func=AF.Identity), rd=[PBb[bk]], wr=half(ktok_b, b))

        def part_b(c):
            b = c % 2
            sm = small[:, b, :]
            smb_ = smallb[b]
            sgv = sgo_ap[:, b, :]
            sgb = half(sgo_b, b)
            T.op("dve", lambda e: e.tensor_tensor(out=sm[:, 12:16], in0=sm[:, 4:8], in1=sm[:, 4:8], op=ALU.mult), rd=[smb_], wr=[smb_])
            T.op("dve", lambda e: e.tensor_tensor(out=sm[:, 12:16], in0=sm[:, 12:16], in1=sm[:, 8:12], op=ALU.mult), rd=[smb_, smallb2[b]], wr=[smb_])
            T.op("dve", lambda e: e.tensor_scalar(out=sm[:, 12:16], in0=sm[:, 12:16], scalar1=1.0 / DV, scalar2=EPS, op0=ALU.mult, op1=ALU.add), rd=[smb_], wr=[smb_])
            T.op("act", lambda e: e.activation(out=sm[:, 12:16], in_=sm[:, 12:16], func=AF.Sqrt), rd=[smb_], wr=[smb_])
            T.op("dve", lambda e: e.reciprocal(out=sm[:, 16:20], in_=sm[:, 12:16]), rd=[smb_], wr=[smb_])
            T.op("dve", lambda e: e.tensor_tensor(out=sm[:, 20:24], in0=sm[:, 16:20], in1=sm[:, 4:8], op=ALU.mult), rd=[smb_], wr=[smb_])
            ypv = ypre_ap[:, b, :]
            ypb = half(ypre_b, b)
            for h in range(4):
                T.op("dve", lambda e, h=h: e.scalar_tensor_tensor(out=ypv[:, h * 256:(h + 1) * 256], in0=numv[h // 2][:, h % 2, :], scalar=sm[:, 20 + h:21 + h],
                                                                  in1=sgv[:, h * 256:(h + 1) * 256], op0=ALU.mult, op1=ALU.mult),
                     rd=[PBb[h // 2], smb_] + sgb, wr=ypb)
            return (c, b)

        pend = None
        pendb = None
        front_part(0)
        for c in range(nsub):
            b = c % 2
            ccols = slice(c * 128, (c + 1) * 128)
            vwv = vw_ap[:, b]
            vwb = half(vw_b, b)
            ktv = ktok_ap[:, b]
            ktb = half(ktok_b, b)
            if not state_only:
                smv = smt_ap[:, b]
                smb = half(smt_b, b)
                qav = qat_ap[:, b]
                qab = half(qat_b, b)
                sgv = sgo_ap[:, b, :]
                sgb = half(sgo_b, b)
            for n in range(2):
                bk = next_pf()
                w, wb = vu[n]
                for kc in range(8):
                    T.op("pe", lambda e, kc=kc: e.matmul(PB[bk][:], lhsT=xT[:, kc, ccols], rhs=w[:, kc, :], start=(kc == 0), stop=(kc == 7)),
                         rd=[wb] + X['b'][c], wr=[PBb[bk]], inc=(kc == 7))
                T.op("dve", lambda e: e.tensor_tensor(out=vwv[:, 2 * n:2 * n + 2, :], in0=PB[bk][:].rearrange("p (h d) -> p h d", d=256),
                                                      in1=wfl[:, c, 2 * n:2 * n + 2].unsqueeze(2).to_broadcast([128, 2, 256]), op=ALU.mult),
                     rd=[PBb[bk], wflb], wr=vwb)
            if pendb is not None:
                pend = part_b(pendb)
                pendb = None
            if not state_only:
                for n in range(2):
                    bk = next_pf()
                    w, wb = ou[n]
                    for kc in range(8):
                        T.op("pe", lambda e, kc=kc: e.matmul(PB[bk][:], lhsT=xT[:, kc, ccols], rhs=w[:, kc, :], start=(kc == 0), stop=(kc == 7)),
                             rd=[wb] + X['b'][c], wr=[PBb[bk]], inc=(kc == 7))
                    T.op("act", lambda e: e.activation(out=sgv[:, n * 512:(n + 1) * 512], in_=PB[bk][:], func=AF.Sigmoid), rd=[PBb[bk]], wr=sgb)
            if pend is not None:
                emit_ypre_T(*pend)
                pend = None
            if not state_only:
                for h in range(4):
                    nb_ = PBb[h // 2]
                    o = numv[h // 2][:, h % 2, :]
                    T.op("pe", lambda e, h=h, o=o: e.matmul(o, lhsT=smv[:, h, :], rhs=vwv[:, h, :], start=True, stop=False), rd=smb + vwb, wr=[nb_], inc=False)
                    T.op("pe", lambda e, h=h, o=o: e.matmul(o, lhsT=qav[:, h, :], rhs=Cb_[:, h, :], start=False, stop=True), rd=qab + [Cbb], wr=[nb_], inc=False)
                    T.op("pe", lambda e, h=h: e.matmul(den[:, h:h + 1], lhsT=smv[:, h, :], rhs=wcol[:, c, h:h + 1], start=True, stop=False), rd=smb + [wcolb], wr=[PBb[5]], inc=False)
                    T.op("pe", lambda e, h=h: e.matmul(den[:, h:h + 1], lhsT=qav[:, h, :], rhs=nbf[:, h:h + 1], start=False, stop=True), rd=qab + [nbfb], wr=[PBb[5]], inc=(h == 3))
            for h in range(4):
                T.op("pe", lambda e, h=h: e.matmul(dcv[h // 2][:, h % 2, :], lhsT=ktv[:, h, :], rhs=vwv[:, h, :], start=True, stop=True), rd=ktb + vwb, wr=[PBb[2 + h // 2]], inc=False)
                T.op("pe", lambda e, h=h: e.matmul(dn[:, h:h + 1], lhsT=ktv[:, h, :], rhs=wcol[:, c, h:h + 1], start=True, stop=True), rd=ktb + [wcolb], wr=[PBb[5]], inc=(h == 3))
            if c + 1 < nsub:
                front_part(c + 1)
            if not state_only:
                sm = small[:, b, :]
                smb_ = smallb[b]
                T.op("dve", lambda e: e.tensor_scalar(out=sm[:, 24:28], in0=den, scalar1=-1.0, scalar2=None, op0=ALU.mult), rd=[PBb[5]], wr=[smb_])
                T.op("dve", lambda e: e.tensor_tensor(out=sm[:, 0:4], in0=sm[:, 24:28], in1=den, op=ALU.max), rd=[PBb[5], smb_], wr=[smb_])
                T.op("dve", lambda e: e.tensor_tensor(out=sm[:, 0:4], in0=sm[:, 0:4], in1=wfl[:, c, 4:8], op=ALU.max), rd=[smb_, wflb], wr=[smb_])
                T.op("dve", lambda e: e.reciprocal(out=sm[:, 4:8], in_=sm[:, 0:4]), rd=[smb_], wr=[smb_])
                for h in range(4):
                    T.op("act", lambda e, h=h: e.activation(out=junk[:], in_=numv[h // 2][:, h % 2, :], func=AF.Square, accum_out=sm[:, 8 + h:9 + h]),
                         rd=[PBb[h // 2]], wr=[junkb, smallb2[b]])
            for h in range(4):
                T.op("dve", lambda e, h=h: e.scalar_tensor_tensor(out=Cst[:, h, :], in0=Cst[:, h, :], scalar=arep[:, 0, 4 * c + h:4 * c + h + 1], in1=dcv[h // 2][:, h % 2, :],
                                                                  op0=ALU.mult, op1=ALU.add), rd=[Cstb, arepb, PBb[2 + h // 2]], wr=[Cstb])
            T.op("dve", lambda e: e.tensor_tensor(out=nst[:], in0=nst[:], in1=arep[:, 0, 4 * c:4 * c + 4], op=ALU.mult), rd=[nstb, arepb], wr=[nstb])
            T.op("dve", lambda e: e.tensor_tensor(out=nst[:], in0=nst[:], in1=dn, op=ALU.add), rd=[nstb, PBb[5]], wr=[nstb])
            T.op("act", lambda e: e.activation(out=Cb_[:].rearrange("p h d -> p (h d)"), in_=Cst[:].rearrange("p h d -> p (h d)"), func=AF.Identity), rd=[Cstb], wr=[Cbb])
            T.op("act", lambda e: e.activation(out=nbf[:], in_=nst[:], func=AF.Identity), rd=[nstb], wr=[nbfb])
            if not state_only:
                pendb = c
        if pendb is not None:
            pend = part_b(pendb)
        if pend is not None:
            emit_ypre_T(*pend)
        ws.release(2 if state_only else 4)

    def xattn_tile(nsub, l):
        Tn = 128 * nsub
        arena_switch('xattn')
        pf_list[0] = [6, 7, 4, 5]
        wqu = [ws.get(("wq", l, 0)), ws.get(("wq", l, 1))]
        for c in range(8):
            w, wb = wqu[c // 4]
            col0 = (c % 4) * 128
            bk = next_pf()
            for kc in range(8):
                T.op("pe", lambda e, kc=kc: e.matmul(PB[bk][:, 0:Tn], lhsT=w[:, kc, col0:col0 + 128], rhs=xT[:, kc, 0:Tn], start=(kc == 0), stop=(kc == 7)),
                     rd=[wb] + xr(XTA, range(nsub)), wr=[PBb[bk]], inc=(kc == 7))
            if c % 2 == 0:
                T.op("act", lambda e: e.activation(out=FA[:, c, 0:Tn], in_=PB[bk][:, 0:Tn], func=AF.Identity, scale=float(256 ** -0.5)), rd=[PBb[bk]], wr=[FAb[c]])
            else:
                T.op("dve", lambda e: e.tensor_scalar(out=FA[:, c, 0:Tn], in0=PB[bk][:, 0:Tn], scalar1=float(256 ** -0.5), scalar2=None, op0=ALU.mult), rd=[PBb[bk]], wr=[FAb[c]])
        ws.release(2)
        pf_list[0] = [6, 7, 4, 5, 2, 3]

        def scores(h):
            ptb = [pt_b[h]]
            for mc in range(2):
                bk = next_pf()
                for hf in range(2):
                    T.op("pe", lambda e, hf=hf: e.matmul(PB[bk][:, 0:Tn], lhsT=KT[:, 2 * h + hf, mc * 128:(mc + 1) * 128], rhs=FA[:, 2 * h + hf, 0:Tn], start=(hf == 0), stop=(hf == 1)),
                         rd=[KTb, FAb[2 * h + hf]], wr=[PBb[bk]], inc=(hf == 1))
                T.op("act", lambda e: e.activation(out=pt_ap[:, h, mc, 0:Tn], in_=PB[bk][:, 0:Tn], func=AF.Exp), rd=[PBb[bk]], wr=ptb)

        scores(0)
        for h in range(4):
            ptb = [pt_b[h]]
            if h < 3:
                scores(h + 1)
            bk = next_pf()
            for mc in range(2):
                T.op("pe", lambda e, mc=mc: e.matmul(PB[bk][:, 0:Tn], lhsT=onesb[:], rhs=pt_ap[:, h, mc, 0:Tn], start=(mc == 0), stop=(mc == 1)), rd=ptb + [cb], wr=[PBb[bk]], inc=(mc == 1))
            rsb = [rs_b[h]]
            T.op("act", lambda e: e.activation(out=rs_ap[:, h, 0:Tn], in_=PB[bk][:, 0:Tn], func=AF.Ln), rd=[PBb[bk]], wr=rsb)
            T.op("act", lambda e: e.activation(out=rs_ap[:, h, 0:Tn], in_=rs_ap[:, h, 0:Tn], func=AF.Exp, scale=-1.0), rd=rsb, wr=rsb)
            for hf in range(2):
                c = 2 * h + hf
                bk = next_pf()
                for mc in range(2):
                    T.op("pe", lambda e, mc=mc: e.matmul(PB[bk][:, 0:Tn], lhsT=Vm[:, mc, c * 128:(c + 1) * 128], rhs=pt_ap[:, h, mc, 0:Tn], start=(mc == 0), stop=(mc == 1)),
                         rd=ptb + [Vmb], wr=[PBb[bk]], inc=(mc == 1))
                T.op("dve", lambda e: e.tensor_tensor(out=FA[:, c, 0:Tn], in0=PB[bk][:, 0:Tn], in1=rs_ap[:, h, 0:Tn], op=ALU.mult), rd=[PBb[bk]] + rsb, wr=[FAb[c]])
        pf_list[0] = [6, 7]

    def mlp_hidden(nsub, l):
        Tn = 128 * nsub
        arena_switch('mlp')
        pf_list[0] = [6, 7, 4, 5]
        for i in range(8):
            w, wb = ws.get(("w1", l, i))
            for ff in range(4):
                f = i * 4 + ff
                bk = next_pf()
                for kc in range(8):
                    T.op("pe", lambda e, kc=kc: e.matmul(PB[bk][:, 0:Tn], lhsT=w[:, kc, ff * 128:(ff + 1) * 128], rhs=xT[:, kc, 0:Tn], start=(kc == 0), stop=(kc == 7)),
                         rd=[wb] + xr(XTA, range(nsub)), wr=[PBb[bk]], inc=(kc == 7))
                k = f % 2
                T.op("act", lambda e: e.activation(out=rtmp[:, k, 0:Tn], in_=PB[bk][:, 0:Tn], func=AF.Relu, bias=b1c[:, l, f:f + 1], scale=1.0), rd=[PBb[bk], cb], wr=[rtmpb[k]])
                T.op("pool", lambda e: e.tensor_tensor(out=hT[:, f, 0:Tn], in0=rtmp[:, k, 0:Tn], in1=rtmp[:, k, 0:Tn], op=ALU.mult), rd=[rtmpb[k]], wr=[hTb[f]])
            ws.release(1)
        pf_list[0] = [6, 7]

    def mlp_proj(l):
        state = {"n": -1, "units": None}

        def proj(s, n, bk, first):
            if state["n"] != n:
                if state["units"] is not None:
                    ws.release(4)
                state["units"] = [ws.get(("w2", l, n, q)) for q in range(4)]
                state["n"] = n
            for f in range(32):
                w, wb = state["units"][f // 8]
                T.op("pe", lambda e, f=f: e.matmul(PB[bk][:], lhsT=hT[:, f, s * 128:(s + 1) * 128], rhs=w[:, f % 8, :], start=(first and f == 0), stop=(f == 31)),
                     rd=[wb, hTb[f]], wr=[PBb[bk]], inc=(f == 31))
        return proj

    def pool_phase(first_tile):
        Tn = 512
        L = 16 + Tn
        for j, wdw in enumerate((2, 4, 8, 16)):
            eng = "dve"
            X = x1T[:, 2 * j:2 * j + 2, :]
            cur, curb = X, [x1b[j]]
            k = 1
            lvl = 0
            while k < wdw:
                nxt = ptmp[:, lvl % 2]
                nb_ = [ptmpb[lvl % 2]]
                lo = 2 * k - 1
                T.op(eng, lambda e, cur=cur, nxt=nxt, lo=lo, k=k: e.tensor_tensor(out=nxt[:, :, lo:L], in0=cur[:, :, lo:L], in1=cur[:, :, lo - k:L - k], op=ALU.add),
                     rd=curb, wr=nb_)
                cur, curb = nxt, nb_
                k *= 2
                lvl += 1
            T.op("dve", lambda e, cur=cur: e.scalar_tensor_tensor(out=FA[:, 2 * j:2 * j + 2, 0:Tn], in0=cur[:, :, 16:L], scalar=1.0 / wdw, in1=X[:, :, 16:L], op0=ALU.mult, op1=ALU.subtract),
                 rd=curb + [x1b[j]], wr=FAb[2 * j:2 * j + 2])
            if first_tile:
                T.op("dve", lambda e, cur=cur: e.tensor_tensor(out=ptiny[:], in0=cur[:, :, 16:32], in1=invct[:, j, :].unsqueeze(1).to_broadcast([128, 2, 16]), op=ALU.mult),
                     rd=curb + [cb], wr=[ptinyb])
                T.op("dve", lambda e: e.tensor_tensor(out=FA[:, 2 * j:2 * j + 2, 0:16], in0=ptiny[:], in1=X[:, :, 16:32], op=ALU.subtract), rd=[ptinyb, x1b[j]], wr=FAb[2 * j:2 * j + 2])

    def pool_proj(s, n, bk, first):
        for jj in range(2):
            j = 2 * n + jj
            for kc in range(2):
                T.op("pe", lambda e, kc=kc: e.matmul(PB[bk][:, jj * 256:(jj + 1) * 256], lhsT=FA[:, 2 * j + kc, s * 128:(s + 1) * 128], rhs=poolw[:, j, kc, :], start=False, stop=(kc == 1)),
                     rd=[FAb[2 * j + kc], poolwb], wr=[PBb[bk]], inc=(jj == 1 and kc == 1))

    def halo_update(Tn, use_flag):
        T.op("pool", lambda e: e.tensor_copy(out=x1T[:, :, 0:16], in_=x1T[:, :, Tn:Tn + 16]), rd=x1b, wr=x1b)
        if use_flag:
            T.op("pool", lambda e: e.tensor_scalar(out=x1T[:, :, 0:16], in0=x1T[:, :, 0:16], scalar1=flg[:, 0:1], scalar2=None, op0=ALU.mult), rd=x1b + [cb], wr=x1b)

    pre_tiles = []
    t0 = 0
    while t0 < n_pre:
        ns = min(4, (n_pre - t0) // 128)
        pre_tiles.append((t0, ns))
        t0 += ns * 128
    main_tiles = [(0, 1, True)] + [(MINI + i * 512, 4, False) for i in range(n_main // 512)]
    ws.plan(["kv0", "kv1", "kv2", "kv3"])
    for _ in pre_tiles:
        ws.plan(["qk1", "v0", "v1"])

    def l_units(l):
        return [("wq", l, 0), ("wq", l, 1), ("wo", l, 0), ("wo", l, 1)] + [("w1", l, i) for i in range(8)] + [("w2", l, n, q) for n in range(2) for q in range(4)]

    for (_, _, mini) in main_tiles:
        ws.plan(["qk0", "qk1", "v0", "v1", "o0", "o1", "wout0", "wout1"] + l_units(0))
        if not mini:
            ws.plan(l_units(1))
    ws.start()

    T.dma(R[:, 0:2, :], memb.rearrange("(s p) d -> p s d", p=128), wr=Rb[0:2])
    for s in range(2):
        feat_major(s, xT, xT_bufs(s))
    kvu = [ws.get("kv%d" % i) for i in range(4)]
    for c in range(8):
        w, wb = kvu[c // 4]
        col0 = (c % 4) * 128
        bk = next_pf()
        for kc in range(8):
            T.op("pe", lambda e, kc=kc: e.matmul(PB[bk][:, 0:256], lhsT=w[:, kc, col0:col0 + 128], rhs=xT[:, kc, 0:256], start=(kc == 0), stop=(kc == 7)),
                 rd=[wb] + xr(XTA, range(2)), wr=[PBb[bk]], inc=(kc == 7))
        T.op("act", lambda e: e.activation(out=KT[:, c, :], in_=PB[bk][:, 0:256], func=AF.Identity), rd=[PBb[bk]], wr=[KTb])
    for mc in range(2):
        for n in range(2):
            w, wb = kvu[2 + n]
            bk = next_pf()
            for kc in range(8):
                T.op("pe", lambda e, kc=kc: e.matmul(PB[bk][:], lhsT=xT[:, kc, mc * 128:(mc + 1) * 128], rhs=w[:, kc, :], start=(kc == 0), stop=(kc == 7)),
                     rd=[wb] + xTb[mc], wr=[PBb[bk]], inc=(kc == 7))
            T.op("dve", lambda e: e.tensor_copy(out=Vm[:, mc, n * 512:(n + 1) * 512], in_=PB[bk][:]), rd=[PBb[bk]], wr=[Vmb])
    ws.release(4)

    i_ps = sg_load(pool_scale[0:1, :])
    T.dma(R[:, 2, :].rearrange("p (j k d) -> p j k d", j=2, k=2), pool_w[0:2].rearrange("j (k p) d -> p j k d", p=128), wr=[Rb[2]])
    T.dma(R[:, 3, :].rearrange("p (j k d) -> p j k d", j=2, k=2), pool_w[2:4].rearrange("j (k p) d -> p j k d", p=128), wr=[Rb[3]])
    for j in range(4):
        src = R[:, 2 + j // 2, :].rearrange("p (j k d) -> p j k d", j=2, k=2)[:, j % 2]
        T.op("dve", lambda e, src=src, j=j: e.tensor_tensor(out=poolw[:, j], in0=src, in1=SG[i_ps][:, j * 256:(j + 1) * 256].unsqueeze(1).to_broadcast([128, 2, 256]), op=ALU.mult),
             rd=[Rb[2 + j // 2], SGb[i_ps]], wr=[poolwb])

    out_st = []
    npre = len(pre_tiles)
    jobs = []
    for i, (tk0, ns) in enumerate(pre_tiles):
        par = (npre - 1 - i) % 2
        jobs.append(dict(src=xp, tk0=tk0, ns=ns, stage=(STR if par == 0 else STX), X=(XTA if par == 0 else XTB), pre=True))
    jobs.append(dict(src=xm, tk0=0, ns=1, stage=STR, X=XTA, pre=False, mini=True))

    def prep(job):
        job["stage"].load(job["src"], job["tk0"], job["ns"])
        for s_ in range(job["ns"]):
            feat_major(s_, job["X"]["ap"], xT_bufs(s_, job["X"]), stage=job["stage"])

    def layer0_rest(ns, res, mini):
        wou = [ws.get("wout0"), ws.get("wout1")]
        ln_stage(ns, tok_proj(FB, lambda s, kc: [FBb[kc]], wou), None, None, 0, dst=xT, dstbufs=xT_bufs, res=res)
        ws.release(2)
        xattn_tile(ns, 0)
        wou = [ws.get(("wo", 0, 0)), ws.get(("wo", 0, 1))]
        ln_stage(ns, tok_proj(FA, lambda s, kc: [FAb[kc]], wou), lng_rows[0:1, :], 0, 1, dst=xT, dstbufs=xT_bufs)
        ws.release(2)
        mlp_hidden(ns, 0)
        ln_stage(ns, mlp_proj(0), lng_rows[1:2, :], 1, 2, dst=x1T, dstbufs=lambda s: (lambda kc: [x1b[kc // 2]]), f32dst=True)
        ws.release(4)

    prep(jobs[0])
    per_tile = (len(cast_jobs) + max(1, npre - 2) - 1) // max(1, npre - 2)
    for i, job in enumerate(jobs):
        ns = job["ns"]
        emit_casts(per_tile)
        mlstm_front(ns, job["pre"], job["X"])
        nxt = jobs[i + 1] if i + 1 < len(jobs) else None
        early = nxt is not None and nxt["X"] is not job["X"] and nxt["stage"] is not job["stage"]
        if early:
            prep(nxt)
        mlstm_loop(ns, job["pre"], job["X"])
        if nxt is not None and not early:
            prep(nxt)
        if job["pre"] and i + 1 < len(jobs) and not jobs[i + 1]["pre"]:
            alias_switch([b for p_ in FXb for b in p_], FBb)
    T.op("dve", lambda e: e.tensor_scalar(out=Cst[:].rearrange("p h d -> p (h d)"), in0=Cst[:].rearrange("p h d -> p (h d)"), scalar1=flg[:, 0:1], scalar2=None, op0=ALU.mult),
         rd=[Cstb, cb], wr=[Cstb])
    T.op("dve", lambda e: e.tensor_scalar(out=nst[:], in0=nst[:], scalar1=flg[:, 0:1], scalar2=None, op0=ALU.mult), rd=[nstb, cb], wr=[nstb])
    T.op("dve", lambda e: e.tensor_scalar(out=gsm[:, 12:13], in0=gsm[:, 12:13], scalar1=flg[0:4, 0:1], scalar2=None, op0=ALU.mult), rd=[gsmb, cb], wr=[gsmb])
    T.op("act", lambda e: e.activation(out=Cb_[:].rearrange("p h d -> p (h d)"), in_=Cst[:].rearrange("p h d -> p (h d)"), func=AF.Identity), rd=[Cstb], wr=[Cbb])
    T.op("act", lambda e: e.activation(out=nbf[:], in_=nst[:], func=AF.Identity), rd=[nstb], wr=[nbfb])
    layer0_rest(1, STR, True)
    halo_update(128, True)

    n_tiles = n_main // 512
    use_stx = cfg.get("stx", True)
    MST = STX if use_stx else STR
    if use_stx:
        STX.load(xm, MINI, 4)
    for ti in range(n_tiles):
        ns, Tn = 4, 512
        if not use_stx:
            STR.load(xm, MINI + ti * 512, 4)
        for s_ in range(ns):
            feat_major(s_, xT, xT_bufs(s_), stage=MST)
        mlstm_front(ns, False, XTA)
        mlstm_loop(ns, False, XTA)
        layer0_rest(ns, MST, False)
        if dbg is not None and ti == dbg.get("_tile", 0):
            dump("l0ln3", R[:, 0:ns, :], Rb[0:ns], [128, ns, D])
        pool_phase(ti == 0)
        ln_stage(ns, pool_proj, lng_rows[2:3, :], 2, 3, dst=xT, dstbufs=xT_bufs)
        halo_update(Tn, False)
        if use_stx and ti + 1 < n_tiles:
            STX.load(xm, MINI + (ti + 1) * 512, 4)
        xattn_tile(ns, 1)
        wou = [ws.get(("wo", 1, 0)), ws.get(("wo", 1, 1))]
        ln_stage(ns, tok_proj(FA, lambda s, kc: [FAb[kc]], wou), lng_rows[3:4, :], 3, 4, dst=xT, dstbufs=xT_bufs)
        ws.release(2)
        mlp_hidden(ns, 1)
        ln_stage(ns, mlp_proj(1), lng_rows[4:5, :], 4, 5, final=True)
        ws.release(4)
        out_st.append(T.dma(out[ti * 512:(ti + 1) * 512, :].rearrange("(s p) d -> p s d", p=128), R[:, 0:ns, :], rd=Rb[0:ns]))

    for key, st in out_st + dumps:
        if T.seen["sp"].get(key, 0) < st:
            T.eng["sp"].wait_ge(T._semof(key), st)
            T.seen["sp"][key] = st
    es.close()
    return nc, T


def host_inputs(x, mem, mlstm_w_in, mlstm_gate_b, mlstm_conv_w, mlstm_norm_g, mlstm_w_out, pool_w, pool_scale, mem_w_kv,
                xattn_wq, xattn_wo, mlp_w1, mlp_b1, mlp_w2, mlp_b2, ln_g, ln_b, n_pre=PRE, n_main=TOK_MAIN, cores=None, seq=SEQ):
    f = lambda a: np.ascontiguousarray(np.asarray(a, dtype=np.float32))
    x, mem = f(x), f(mem)
    ln_g, ln_b = f(ln_g), f(ln_b)
    col = lambda v: np.ascontiguousarray(v.reshape(8, 128).T)
    lng6 = ln_g.reshape(6, D)
    lnb6 = ln_b.reshape(6, D)
    rows8 = np.zeros((128, D), np.float32)
    rows8[0:5] = lnb6[0:5]
    rows8[5] = f(mlp_b2)[0]
    rows8[6] = f(mlp_b2)[1]
    sel = np.zeros((128, 6, 128), np.float32)
    a = np.float32(ALPHA)
    sel[0, 1] = a
    sel[1, 2] = a
    sel[5, 2] = 1.0
    sel[2, 3] = a
    sel[3, 4] = a
    sel[4, 5] = a
    sel[6, 5] = 1.0
    mask = np.zeros((128, 4, 128), np.float32)
    tri = (np.arange(128)[None, :] >= np.arange(128)[:, None]).astype(np.float32)
    mask[:] = tri[:, None, :]
    shared = {
        "w_in": f(mlstm_w_in)[0], "w_out": f(mlstm_w_out)[0], "w_kv": f(mem_w_kv), "wq": f(xattn_wq), "wo": f(xattn_wo),
        "w1": f(mlp_w1), "w2": f(mlp_w2), "pool_w": f(pool_w)[0], "pool_scale": f(pool_scale).reshape(1, D),
        "convw": np.ascontiguousarray(f(mlstm_conv_w)[0].reshape(4, 8, 128).transpose(2, 1, 0)),
        "normg": col(f(mlstm_norm_g)[0]),
        "lng_col": np.ascontiguousarray(np.stack([col(lng6[i]) for i in range(6)], axis=1)),
        "lnb_col": np.ascontiguousarray(np.stack([col(lnb6[i]) for i in range(6)], axis=1)),
        "lng_rows": lng6, "lnb_rows": lnb6,
        "b1_col": np.ascontiguousarray(f(mlp_b1).reshape(2, 32, 128).transpose(2, 0, 1)),
        "gateb": np.ascontiguousarray(f(mlstm_gate_b)[0].reshape(2, 4).T),
        "rows8": rows8, "sel": sel, "ident": np.eye(128, dtype=np.float32), "mask": mask,
    }
    half = seq // 2
    maps = []
    wins = np.array([2, 4, 8, 16], np.float32)
    for core in (range(8) if cores is None else cores):
        b, hf = core // 2, core % 2
        m = dict(shared)
        if hf == 0:
            xmain = np.concatenate([np.zeros((MINI, D), np.float32), x[b, 0:n_main]], axis=0)
            xpre = np.zeros((max(n_pre, 128), D), np.float32)
            cnt = np.minimum(np.arange(16, dtype=np.float32)[None, :] + 1.0, wins[:, None])
            fl = 0.0
        else:
            xmain = x[b, half - MINI: half + n_main]
            xpre = x[b, half - MINI - n_pre: half - MINI] if n_pre > 0 else np.zeros((128, D), np.float32)
            cnt = np.broadcast_to(wins[:, None], (4, 16))
            fl = 1.0
        m["xm"] = np.ascontiguousarray(xmain)
        m["xp"] = np.ascontiguousarray(xpre)
        m["memb"] = mem[b]
        m["flag"] = np.full((128, 1), fl, np.float32)
        m["invc"] = np.ascontiguousarray(np.broadcast_to((1.0 / cnt)[None], (128, 4, 16)).astype(np.float32))
        maps.append(m)
    return maps


_CACHE = {}


def kernel(**inputs):
    if "nc" not in _CACHE:
        _CACHE["nc"] = build({})[0]
    nc = _CACHE["nc"]
    maps = host_inputs(**inputs)
    res = run_bass_kernel_spmd(nc, maps, core_ids=list(range(8)))
    out = np.empty((BATCH, SEQ, D), np.float32)
    for core in range(8):
        b, hf = core // 2, core % 2
        out[b, hf * (SEQ // 2):(hf + 1) * (SEQ // 2)] = res.results[core]["out"]
    return out
```
